# Optimizing a Trainium2 kernel written in Bass

```python
import jax, jax.numpy as jnp
from jax import lax
import numpy as np

D_MODEL = 1024
BATCH = 2
SEQ = 16384
DEPTH = 2

CONV_DIM = 512
CONV_WIDTH = 3
NSA_HEADS = 8
NSA_GROUPS = 2
HEADS_PER_GROUP = NSA_HEADS // NSA_GROUPS
HEAD_DIM = 64
ROT_DIM = HEAD_DIM // 4
ROPE_THETA = 500000.0
CMP_BLOCK = 32
CMP_STRIDE = 16
CMP_HIDDEN = 256
SLC_BLOCK = 64
N_SELECT = 16
WINDOW = 512
MEM_LEN = 256
MEM_HEADS = 4
MEM_HEAD_DIM = 128
D_FF = 2816
Q_BLOCK = 128
EPS = 1e-6
NEG = -1e30
BIG = 1e30
SPLIT_SIZES = (CONV_DIM, CONV_DIM, CONV_DIM, NSA_HEADS * HEAD_DIM, 6 * NSA_GROUPS * HEAD_DIM, 3 * NSA_HEADS, MEM_HEADS * MEM_HEAD_DIM, 3 * D_MODEL)
MIX_IN = 3 * CONV_DIM + NSA_HEADS * HEAD_DIM + 6 * NSA_GROUPS * HEAD_DIM + 3 * NSA_HEADS + MEM_HEADS * MEM_HEAD_DIM + 3 * D_MODEL

kernel_name = 'hybrid_conv_nsa_memory_macaron_block'


def rms_norm(x, g):
    xf = x.astype(jnp.float32)
    y = xf * lax.rsqrt(jnp.mean(xf * xf, axis=-1, keepdims=True) + EPS)
    return (y * g.astype(jnp.float32)).astype(x.dtype)


def swiglu(h, w_in, w_out):
    a, b = jnp.split(h @ w_in, 2, axis=-1)
    return (jax.nn.silu(a) * b) @ w_out


def rope_tables(positions):
    inv_freq = ROPE_THETA ** (-jnp.arange(0, ROT_DIM, 2, dtype=jnp.float32) / ROT_DIM)
    ang = positions.astype(jnp.float32)[..., None] * inv_freq
    return jnp.cos(ang)[:, :, None, :], jnp.sin(ang)[:, :, None, :]


def partial_rope(t, cos, sin):
    half = ROT_DIM // 2
    t1, t2, rest = t[..., :half], t[..., half:ROT_DIM], t[..., ROT_DIM:]
    return jnp.concatenate([t1 * cos - t2 * sin, t2 * cos + t1 * sin, rest], axis=-1)


def masked_softmax(s, mask):
    s = jnp.where(mask, s.astype(jnp.float32), NEG)
    m = jnp.max(s, axis=-1, keepdims=True)
    p = jnp.where(mask, jnp.exp(s - m), 0.0)
    return p / jnp.maximum(jnp.sum(p, axis=-1, keepdims=True), 1e-30)


def short_conv(v, w):
    return lax.conv_general_dilated(v, w[:, None, :].astype(v.dtype), window_strides=(1,),
                                    padding=[(CONV_WIDTH - 1, 0)],
                                    dimension_numbers=('NWC', 'WIO', 'NWC'),
                                    feature_group_count=v.shape[-1])


def compress(t, pos_emb, w1, b1, w2):
    b, s, g, d = t.shape
    chunks = t.reshape(b, s // CMP_STRIDE, CMP_STRIDE, g, d)
    blocks = jnp.concatenate([chunks[:, :-1], chunks[:, 1:]], axis=2)
    blocks = blocks + pos_emb[None, None, :, None, :]
    flat = jnp.moveaxis(blocks, 3, 2).reshape(b, -1, g, CMP_BLOCK * d)
    return jax.nn.gelu(flat @ w1 + b1) @ w2


def cmp_to_slc_matrix(n_cmp, n_slc):
    i = jnp.arange(n_cmp)[:, None] * CMP_STRIDE
    j = jnp.arange(n_slc)[None, :] * SLC_BLOCK
    ov = jnp.minimum(i + CMP_BLOCK, j + SLC_BLOCK) - jnp.maximum(i, j)
    return jnp.maximum(ov, 0).astype(jnp.float32) / CMP_BLOCK


def nsa_attention(q, q_rot, k_cmp, v_cmp, k_slc, v_slc, k_win, v_win, gates):
    b, s, h, d = q.shape
    n_cmp = k_cmp.shape[1]
    n_slc = s // SLC_BLOCK
    top = min(N_SELECT, n_slc)
    scale = HEAD_DIM ** -0.5
    q_plain_g = q.reshape(b, s, NSA_GROUPS, HEADS_PER_GROUP, d)
    q_rot_g = q_rot.reshape(b, s, NSA_GROUPS, HEADS_PER_GROUP, d)
    kc = jnp.transpose(k_cmp, (0, 2, 1, 3))
    vc = jnp.transpose(v_cmp, (0, 2, 1, 3))
    cmp_end = jnp.arange(n_cmp) * CMP_STRIDE + CMP_BLOCK - 1
    overlap = cmp_to_slc_matrix(n_cmp, n_slc)
    ks_blocks = k_slc.reshape(b, n_slc, SLC_BLOCK, NSA_GROUPS, d).transpose(0, 3, 1, 2, 4)
    vs_blocks = v_slc.reshape(b, n_slc, SLC_BLOCK, NSA_GROUPS, d).transpose(0, 3, 1, 2, 4)
    pad = ((0, 0), (WINDOW, 0), (0, 0), (0, 0))
    kw_pad = jnp.pad(k_win, pad)
    vw_pad = jnp.pad(v_win, pad)
    bi = jnp.arange(b)[:, None, None, None]
    gi = jnp.arange(NSA_GROUPS)[None, :, None, None]
    blk = jnp.arange(n_slc)

    def block(start):
        t = start + jnp.arange(Q_BLOCK)
        qp = lax.dynamic_slice_in_dim(q_plain_g, start, Q_BLOCK, axis=1)
        qr = lax.dynamic_slice_in_dim(q_rot_g, start, Q_BLOCK, axis=1)
        s_c = jnp.einsum('bqghd,bgnd->bghqn', qp, kc) * scale
        p_c = masked_softmax(s_c, cmp_end[None, :] <= t[:, None])
        o_c = jnp.einsum('bghqn,bgnd->bqghd', p_c.astype(vc.dtype), vc)
        imp = jnp.einsum('bghqn,nj->bgqj', p_c, overlap)
        cur = (t // SLC_BLOCK)[:, None]
        forced = (blk == 0) | (blk == cur) | (blk == cur - 1)
        imp = jnp.where(blk > cur, NEG, jnp.where(forced, BIG, imp))
        _, idx = lax.top_k(imp, top)
        ks = ks_blocks[bi, gi, idx].reshape(b, NSA_GROUPS, Q_BLOCK, top * SLC_BLOCK, d)
        vs = vs_blocks[bi, gi, idx].reshape(b, NSA_GROUPS, Q_BLOCK, top * SLC_BLOCK, d)
        kpos = (idx[..., None] * SLC_BLOCK + jnp.arange(SLC_BLOCK)).reshape(b, NSA_GROUPS, Q_BLOCK, top * SLC_BLOCK)
        m_s = kpos <= t[None, None, :, None]
        s_s = jnp.einsum('bqghd,bgqld->bghql', qr, ks) * scale
        p_s = masked_softmax(s_s, m_s[:, :, None])
        o_s = jnp.einsum('bghql,bgqld->bqghd', p_s.astype(vs.dtype), vs)
        kw = lax.dynamic_slice_in_dim(kw_pad, start, WINDOW + Q_BLOCK, axis=1)
        vw = lax.dynamic_slice_in_dim(vw_pad, start, WINDOW + Q_BLOCK, axis=1)
        wpos = start - WINDOW + jnp.arange(WINDOW + Q_BLOCK)
        m_w = (wpos[None, :] >= 0) & (wpos[None, :] <= t[:, None]) & (wpos[None, :] > t[:, None] - WINDOW)
        s_w = jnp.einsum('bqghd,bkgd->bghqk', qr, kw) * scale
        p_w = masked_softmax(s_w, m_w)
        o_w = jnp.einsum('bghqk,bkgd->bqghd', p_w.astype(vw.dtype), vw)
        g = lax.dynamic_slice_in_dim(gates, start, Q_BLOCK, axis=1)
        return g[..., 0:1] * o_c + g[..., 1:2] * o_s + g[..., 2:3] * o_w

    out = lax.map(block, jnp.arange(s // Q_BLOCK) * Q_BLOCK)
    return jnp.moveaxis(out, 0, 1).reshape(b, s, h * d)


def token_mixer(h, mem, cos, sin, mem_g, w_in, conv_w, cmp_pos_k, cmp_pos_v,
                ck_w1, ck_b1, ck_w2, cv_w1, cv_b1, cv_w2,
                w_mem_kv, w_br_conv, w_br_nsa, w_br_mem, w_out):
    b, s, _ = h.shape
    points, acc = [], 0
    for n in SPLIT_SIZES[:-1]:
        acc += n
        points.append(acc)
    u, bg, cg, q, kv, nsa_g, q_mem, merge_g = jnp.split(h @ w_in, points, axis=-1)
    cos = cos.astype(h.dtype)
    sin = sin.astype(h.dtype)
    y_conv = bg * short_conv(cg * u, conv_w)
    q = q.reshape(b, s, NSA_HEADS, HEAD_DIM)
    kv = kv.reshape(b, s, 6, NSA_GROUPS, HEAD_DIM)
    k_cmp = compress(kv[:, :, 0], cmp_pos_k, ck_w1, ck_b1, ck_w2)
    v_cmp = compress(kv[:, :, 1], cmp_pos_v, cv_w1, cv_b1, cv_w2)
    k_slc = partial_rope(kv[:, :, 2], cos, sin)
    k_win = partial_rope(kv[:, :, 4], cos, sin)
    q_rot = partial_rope(q, cos, sin)
    gates = jax.nn.sigmoid(nsa_g).reshape(b, s, NSA_GROUPS, HEADS_PER_GROUP, 3)
    y_nsa = nsa_attention(q, q_rot, k_cmp, v_cmp, k_slc, kv[:, :, 3], k_win, kv[:, :, 5], gates)
    m_len = mem.shape[1]
    mkv = (rms_norm(mem, mem_g) @ w_mem_kv).reshape(b, m_len, 2, MEM_HEADS, MEM_HEAD_DIM)
    q_m = q_mem.reshape(b, s, MEM_HEADS, MEM_HEAD_DIM)
    s_m = jnp.einsum('bshd,bmhd->bhsm', q_m, mkv[:, :, 0]).astype(jnp.float32) * (MEM_HEAD_DIM ** -0.5)
    p_m = jax.nn.softmax(s_m, axis=-1)
    y_mem = jnp.einsum('bhsm,bmhd->bshd', p_m.astype(mkv.dtype), mkv[:, :, 1]).reshape(b, s, MEM_HEADS * MEM_HEAD_DIM)
    g_conv, g_nsa, g_mem = jnp.split(jax.nn.sigmoid(merge_g), 3, axis=-1)
    merged = g_conv * (y_conv @ w_br_conv) + g_nsa * (y_nsa @ w_br_nsa) + g_mem * (y_mem @ w_br_mem)
    return merged @ w_out


def setup_inputs(seed: int = 0) -> dict:
    key = jax.random.key(seed)
    keys = iter(jax.random.split(key, 40))
    L = DEPTH

    def dense(shape, fan_in):
        return jax.random.normal(next(keys), shape, jnp.float32) * fan_in ** -0.5

    def gain(shape):
        return 1.0 + 0.05 * jax.random.normal(next(keys), shape, jnp.float32)

    def small(shape, scale):
        return scale * jax.random.normal(next(keys), shape, jnp.float32)

    x = jax.random.normal(next(keys), (BATCH, SEQ, D_MODEL), jnp.float32)
    mem = jax.random.normal(next(keys), (BATCH, MEM_LEN, D_MODEL), jnp.float32)
    offset = jax.random.randint(next(keys), (BATCH, 1), 0, 4096, dtype=jnp.int32)
    positions = (jnp.arange(SEQ, dtype=jnp.int32)[None, :] + offset).astype(jnp.int32)
    return {
        'x': x,
        'mem': mem,
        'positions': positions,
        'ffn1_norm_pre': gain((L, D_MODEL)),
        'ffn1_norm_post': gain((L, D_MODEL)),
        'ffn1_w_in': dense((L, D_MODEL, 2 * D_FF), D_MODEL),
        'ffn1_w_out': dense((L, D_FF, D_MODEL), D_FF),
        'mix_norm_pre': gain((L, D_MODEL)),
        'mix_norm_post': gain((L, D_MODEL)),
        'mem_norm': gain((L, D_MODEL)),
        'w_mix_in': dense((L, D_MODEL, MIX_IN), D_MODEL),
        'conv_w': dense((L, CONV_WIDTH, CONV_DIM), CONV_WIDTH),
        'cmp_pos_k': small((L, CMP_BLOCK, HEAD_DIM), 0.1),
        'cmp_pos_v': small((L, CMP_BLOCK, HEAD_DIM), 0.1),
        'cmp_k_w1': dense((L, CMP_BLOCK * HEAD_DIM, CMP_HIDDEN), CMP_BLOCK * HEAD_DIM),
        'cmp_k_b1': small((L, CMP_HIDDEN), 0.01),
        'cmp_k_w2': dense((L, CMP_HIDDEN, HEAD_DIM), CMP_HIDDEN),
        'cmp_v_w1': dense((L, CMP_BLOCK * HEAD_DIM, CMP_HIDDEN), CMP_BLOCK * HEAD_DIM),
        'cmp_v_b1': small((L, CMP_HIDDEN), 0.01),
        'cmp_v_w2': dense((L, CMP_HIDDEN, HEAD_DIM), CMP_HIDDEN),
        'w_mem_kv': dense((L, D_MODEL, 2 * MEM_HEADS * MEM_HEAD_DIM), D_MODEL),
        'w_branch_conv': dense((L, CONV_DIM, D_MODEL), CONV_DIM),
        'w_branch_nsa': dense((L, NSA_HEADS * HEAD_DIM, D_MODEL), NSA_HEADS * HEAD_DIM),
        'w_branch_mem': dense((L, MEM_HEADS * MEM_HEAD_DIM, D_MODEL), MEM_HEADS * MEM_HEAD_DIM),
        'w_mix_out': dense((L, D_MODEL, D_MODEL), D_MODEL),
        'ffn2_norm_pre': gain((L, D_MODEL)),
        'ffn2_norm_post': gain((L, D_MODEL)),
        'ffn2_w_in': dense((L, D_MODEL, 2 * D_FF), D_MODEL),
        'ffn2_w_out': dense((L, D_FF, D_MODEL), D_FF),
    }


def reference(x, mem, positions, ffn1_norm_pre, ffn1_norm_post, ffn1_w_in, ffn1_w_out,
              mix_norm_pre, mix_norm_post, mem_norm, w_mix_in, conv_w,
              cmp_pos_k, cmp_pos_v, cmp_k_w1, cmp_k_b1, cmp_k_w2, cmp_v_w1, cmp_v_b1, cmp_v_w2,
              w_mem_kv, w_branch_conv, w_branch_nsa, w_branch_mem, w_mix_out,
              ffn2_norm_pre, ffn2_norm_post, ffn2_w_in, ffn2_w_out):
    cos, sin = rope_tables(positions)
    for l in range(DEPTH):
        h = rms_norm(x, ffn1_norm_pre[l])
        x = x + 0.5 * rms_norm(swiglu(h, ffn1_w_in[l], ffn1_w_out[l]), ffn1_norm_post[l])
        h = rms_norm(x, mix_norm_pre[l])
        y = token_mixer(h, mem, cos, sin, mem_norm[l], w_mix_in[l], conv_w[l],
                        cmp_pos_k[l], cmp_pos_v[l], cmp_k_w1[l], cmp_k_b1[l], cmp_k_w2[l],
                        cmp_v_w1[l], cmp_v_b1[l], cmp_v_w2[l], w_mem_kv[l],
                        w_branch_conv[l], w_branch_nsa[l], w_branch_mem[l], w_mix_out[l])
        x = x + rms_norm(y, mix_norm_post[l])
        h = rms_norm(x, ffn2_norm_pre[l])
        x = x + 0.5 * rms_norm(swiglu(h, ffn2_w_in[l], ffn2_w_out[l]), ffn2_norm_post[l])
    return x
```

```python
import os
import numpy as np
import ml_dtypes
from contextlib import ExitStack
import concourse.bass as bass
import concourse.mybir as mybir
from concourse.bass_utils import run_bass_kernel_spmd

F32 = mybir.dt.float32
BF16 = mybir.dt.bfloat16
I32 = mybir.dt.int32
AF = mybir.ActivationFunctionType
ALU = mybir.AluOpType
AX = mybir.AxisListType

D = 1024
DFF = 2816
NT = 4096
MT = 512
NMT = NT // MT
MIX_IN = 6424
C_U, C_B, C_C, C_Q, C_KV, C_NG, C_QM, C_MG = 0, 512, 1024, 1536, 2048, 2816, 2840, 3352
EPS = 1e-6
NEGM = -30000.0


class Trk:
    __slots__ = ("name", "w", "r")

    def __init__(self, name=""):
        self.name = name
        self.w = None
        self.r = []


class Instr:
    __slots__ = ("eng", "fn", "waits", "signal", "sem", "val", "is_dma")

    def __init__(self, eng, fn, is_dma=False):
        self.eng = eng
        self.fn = fn
        self.waits = []
        self.signal = False
        self.sem = None
        self.val = None
        self.is_dma = is_dma


COMPUTE = ("pe", "act", "dve", "pool")


class Prog:
    def __init__(self, nc):
        self.nc = nc
        self.es = ExitStack()
        self.streams = {e: [] for e in ("pe", "act", "dve", "pool", "sp")}
        self.dma_sem_of = {}
        self.sb_bytes = 0

    def sbuf(self, name, shape, dt):
        n = 1
        for s in shape[1:]:
            n *= s
        self.sb_bytes += n * (4 if dt in (F32, I32) else 2)
        return self.es.enter_context(self.nc.sbuf_tensor(name, list(shape), dt))

    def psum(self, name, shape, dt=F32):
        return self.es.enter_context(self.nc.psum_tensor(name, list(shape), dt))

    def _deps(self, ins, reads, writes):
        deps = []
        for t in reads:
            if t.w is not None:
                deps.append((t.w, "raw"))
        for t in writes:
            if t.w is not None:
                deps.append((t.w, "waw"))
            for r in t.r:
                deps.append((r, "war"))
        for d, kind in deps:
            if d is ins:
                continue
            if d.eng == ins.eng and not d.is_dma and not ins.is_dma:
                if kind != "raw" or ins.eng == "pe":
                    continue
            ins.waits.append(d)
            d.signal = True
        for t in reads:
            t.r.append(ins)
        for t in writes:
            t.w = ins
            t.r = []

    def op(self, eng, fn, reads=(), writes=()):
        ins = Instr(eng, fn)
        self._deps(ins, reads, writes)
        self.streams[eng].append(ins)
        return ins

    def dma(self, queue, out, in_, reads=(), writes=(), **kw):
        def fn(e, out=out, in_=in_, kw=kw):
            return e.dma_start(out=out, in_=in_, **kw)
        ins = Instr(queue, fn, is_dma=True)
        self._deps(ins, reads, writes)
        ins.signal = True
        key = writes[0]
        if key not in self.dma_sem_of:
            self.dma_sem_of[key] = [len(self.dma_sem_of), 0]
        rec = self.dma_sem_of[key]
        rec[1] += 16
        ins.sem = ("dma", rec[0])
        ins.val = rec[1]
        self.streams[queue].append(ins)
        return ins

    def barrier(self):
        deps = []
        for e in COMPUTE:
            comp = [x for x in self.streams[e] if not x.is_dma]
            if comp:
                deps.append(comp[-1])
        start = getattr(self, "_bar_pos", {})
        for e, st in self.streams.items():
            for x in st[start.get(e, 0):]:
                if x.is_dma:
                    deps.append(x)
        for e in self.streams:
            ins = Instr(e, lambda eng: eng.nop())
            for d in deps:
                if d.eng == e and not d.is_dma:
                    continue
                ins.waits.append(d)
                d.signal = True
            self.streams[e].append(ins)
        self._bar_pos = {e: len(st) for e, st in self.streams.items()}

    def emit(self, final_waits=()):
        nc = self.nc
        es = self.es
        sems = {e: es.enter_context(nc.semaphore("s_" + e)) for e in COMPUTE}
        dsems = [es.enter_context(nc.semaphore("d%d" % i)) for i in range(len(self.dma_sem_of))]
        for e in COMPUTE:
            c = 0
            for ins in self.streams[e]:
                if ins.is_dma:
                    continue
                if ins.signal:
                    c += 1
                    ins.sem = ("c", e)
                    ins.val = c

        def semh(s):
            return sems[s[1]] if s[0] == "c" else dsems[s[1]]
        block = es.enter_context(nc.Block())
        engmap = {"pe": "tensor", "act": "scalar", "dve": "vector", "pool": "gpsimd", "sp": "sync"}
        final_waits = list(final_waits)

        def make(ename):
            def body(eng):
                waited = {}
                for ins in self.streams[ename]:
                    need = {}
                    for d in ins.waits:
                        if d.val > need.get(d.sem, 0):
                            need[d.sem] = d.val
                    for k, v in need.items():
                        if waited.get(k, 0) >= v:
                            continue
                        eng.wait_ge(semh(k), v)
                        waited[k] = v
                    r = ins.fn(eng)
                    if ins.signal:
                        r.then_inc(semh(ins.sem), 16 if ins.is_dma else 1)
                if ename == "sp":
                    for d in final_waits:
                        eng.wait_ge(semh(d.sem), d.val)
            return body
        for ename, attr in engmap.items():
            getattr(block, attr)(make(ename))
        return {e: len(s) for e, s in self.streams.items()}

    def close(self):
        self.es.close()


class Ctx:
    def __init__(self, P, W, wslots=None, wsize=6144):
        self.P = P
        self.W = W
        self.wsize = wsize
        if wslots is None:
            wslots = [P.sbuf("wslot%d" % i, [128, wsize], BF16) for i in range(3)]
        self.set_wslots(wslots, wsize)
        self.NB = 8
        self.bank = [P.psum("bank%d" % i, [128, 512], F32) for i in range(self.NB)]
        self.btrk = [Trk("b%d" % i) for i in range(self.NB)]
        self.bi = 0
        self.npool = self.NB
        self.ident_f = P.sbuf("ident_f", [128, 128], F32)
        self.ident_b = P.sbuf("ident_b", [128, 128], BF16)
        self.ones_b = P.sbuf("ones_b", [128, 128], BF16)
        self.ones1 = P.sbuf("ones1", [128, 128], BF16)
        self.t_const = Trk("const")
        P.dma("sp", self.ident_f[:], W["ident"], writes=[self.t_const])
        P.op("dve", lambda e: e.tensor_copy(out=self.ident_b[:], in_=self.ident_f[:]),
             reads=[self.t_const], writes=[self.t_const])
        P.op("dve", lambda e: e.memset(self.ones_b[:], 1.0 / 1024.0), writes=[self.t_const])
        P.op("dve", lambda e: e.memset(self.ones1[:], 1.0), writes=[self.t_const])
        self.alt = 0

    def set_wslots(self, wslots, wsize):
        self.wslot = wslots
        self.wsize = wsize
        self.NW = len(wslots)
        self.wtrk = [Trk("w%d" % i) for i in range(self.NW)]
        self.wi = 0

    def set_pool(self, npool):
        self.npool = npool
        self.bi = 0

    def next_w(self):
        i = self.wi
        self.wi = (self.wi + 1) % self.NW
        return self.wslot[i], self.wtrk[i]

    def next_bank(self):
        i = self.bi
        self.bi = (self.bi + 1) % self.npool
        return self.bank[i], self.btrk[i]

    def load_panel(self, src_ap, shape):
        slot, trk = self.next_w()
        slot = slot[:]
        n = 1
        for s in shape[1:]:
            n *= s
        assert n <= self.wsize
        view = slot[:, 0:n]
        if len(shape) == 3:
            view = view.rearrange("p (a b) -> p a b", a=shape[1])
        elif len(shape) == 4:
            view = view.rearrange("p (a b c) -> p a b c", a=shape[1], b=shape[2])
        self.P.dma("pool", view, src_ap, writes=[trk])
        return view, trk

    def evac_engine(self):
        self.alt ^= 1
        return "act" if self.alt else "dve"


def copy_op(P, eng, out, in_, reads, writes):
    if eng == "act":
        return P.op("act", lambda e: e.copy(out=out, in_=in_), reads=reads, writes=writes)
    return P.op(eng, lambda e: e.tensor_copy(out=out, in_=in_), reads=reads, writes=writes)


def load_vec_fm(P, dst, src_row, trk):
    P.dma("sp", dst, src_row.rearrange("(c p) -> p c", p=128), writes=[trk], allow_slow_non_contiguous=True)


def palloc(P):
    def f(name, shape, dt):
        return P.sbuf(name, shape, dt)[:]
    return f


def trks(n, name):
    return [Trk("%s%d" % (name, i)) for i in range(n)]


def load_vec_fm(P, dst, src_row, trk):
    P.dma("sp", dst, src_row.rearrange("(c p) -> p c", p=128), writes=[trk], allow_slow_non_contiguous=True)


class Scr:
    def __init__(self, P):
        self.sq = P.sbuf("sq", [128, 8, MT], BF16)
        self.t_sq = trks(8, "sq")
        self.rstd = P.sbuf("rstd", [128, MT], F32)
        self.t_rstd = Trk("rstd")
        self.sa = P.sbuf("sa", [128, 2, MT], BF16)
        self.t_sa = trks(2, "sa")
        self.eps = P.sbuf("eps", [128, 1], F32)
        self.t_eps = Trk("eps")
        P.op("dve", lambda e: e.memset(self.eps[:], EPS), writes=[self.t_eps])


def _rstd_from(cx, scr, src, t_src, ntok, nch):
    P = cx.P
    bank, t_b = cx.next_bank()
    for c in range(nch):
        P.op("act", lambda e, c=c: e.activation(out=scr.sq[:, c, 0:ntok], in_=src[:, c, 0:ntok], func=AF.Square),
             reads=[t_src[c]], writes=[scr.t_sq[c]])
    for c in range(nch):
        P.op("pe", lambda e, c=c: e.matmul(bank[:, 0:ntok], cx.ones_b[:], scr.sq[:, c, 0:ntok],
                                           start=(c == 0), stop=(c == nch - 1)),
             reads=[scr.t_sq[c], cx.t_const], writes=[t_b])
    P.op("act", lambda e: e.activation(out=scr.rstd[:, 0:ntok], in_=bank[:, 0:ntok], func=AF.Sqrt,
                                       bias=scr.eps[:, 0:1], scale=1.0),
         reads=[t_b, scr.t_eps], writes=[scr.t_rstd])
    P.op("dve", lambda e: e.reciprocal(out=scr.rstd[:, 0:ntok], in_=scr.rstd[:, 0:ntok]),
         reads=[scr.t_rstd], writes=[scr.t_rstd])


def rmsnorm_fm(cx, scr, xT, t_x, gvec, t_g, hT, t_h, ntok=MT, nch=8):
    P = cx.P
    _rstd_from(cx, scr, xT, t_x, ntok, nch)
    for c in range(nch):
        P.op("dve", lambda e, c=c: e.scalar_tensor_tensor(out=hT[:, c, 0:ntok], in0=xT[:, c, 0:ntok],
                                                         scalar=gvec[:, c:c + 1], in1=scr.rstd[:, 0:ntok],
                                                         op0=ALU.mult, op1=ALU.mult),
             reads=[t_x[c], t_g, scr.t_rstd], writes=[t_h[c]])


def sandwich_residual(cx, scr, yT, t_y, xT, t_x, gvec_half, t_g, ntok=MT):
    P = cx.P
    _rstd_from(cx, scr, yT, t_y, ntok, 8)
    for c in range(8):
        P.op("dve", lambda e, c=c: e.scalar_tensor_tensor(out=yT[:, c, 0:ntok], in0=yT[:, c, 0:ntok],
                                                         scalar=gvec_half[:, c:c + 1], in1=scr.rstd[:, 0:ntok],
                                                         op0=ALU.mult, op1=ALU.mult),
             reads=[t_y[c], t_g, scr.t_rstd], writes=[t_y[c]])
        P.op("pool", lambda e, c=c: e.tensor_tensor(out=xT[:, c, 0:ntok], in0=xT[:, c, 0:ntok],
                                                    in1=yT[:, c, 0:ntok], op=ALU.add),
             reads=[t_y[c], t_x[c]], writes=[t_x[c]])


def ffn(cx, scr, hT, t_h, w_in, w_out, hid, t_hid, yT, t_y):
    P = cx.P
    w_in_v = w_in.rearrange("(kc p) n -> p kc n", p=128)
    for n in range(11):
        wv, t_w = cx.load_panel(w_in_v[:, :, n * 256:(n + 1) * 256], [128, 8, 256])
        wv2, t_w2 = cx.load_panel(w_in_v[:, :, DFF + n * 256:DFF + (n + 1) * 256], [128, 8, 256])
        for ci in range(2):
            bA, t_A = cx.next_bank()
            bB, t_B = cx.next_bank()
            for kc in range(8):
                P.op("pe", lambda e, kc=kc, ci=ci, bA=bA, wv=wv: e.matmul(
                    bA[:, 0:MT], wv[:, kc, ci * 128:(ci + 1) * 128], hT[:, kc, :], start=(kc == 0), stop=(kc == 7)),
                    reads=[t_w, t_h[kc]], writes=[t_A])
            for kc in range(8):
                P.op("pe", lambda e, kc=kc, ci=ci, bB=bB, wv2=wv2: e.matmul(
                    bB[:, 0:MT], wv2[:, kc, ci * 128:(ci + 1) * 128], hT[:, kc, :], start=(kc == 0), stop=(kc == 7)),
                    reads=[t_w2, t_h[kc]], writes=[t_B])
            j = 2 * n + ci
            s = j % 2
            P.op("act", lambda e, bA=bA, s=s: e.activation(out=scr.sa[:, s, :], in_=bA[:, 0:MT], func=AF.Silu),
                 reads=[t_A], writes=[scr.t_sa[s]])
            P.op("dve", lambda e, bB=bB, s=s, j=j: e.tensor_tensor(
                out=hid[:, j, :], in0=scr.sa[:, s, :], in1=bB[:, 0:MT], op=ALU.mult),
                reads=[scr.t_sa[s], t_B], writes=[t_hid[j]])
    w_out_v = w_out.rearrange("(kc p) n -> p kc n", p=128)
    for n in range(4):
        wv, t_w = cx.load_panel(w_out_v[:, :, n * 256:(n + 1) * 256], [128, 22, 256])
        for ci in range(2):
            bY, t_Y = cx.next_bank()
            for kc in range(22):
                P.op("pe", lambda e, kc=kc, ci=ci, bY=bY, wv=wv: e.matmul(
                    bY[:, 0:MT], wv[:, kc, ci * 128:(ci + 1) * 128], hid[:, kc, :], start=(kc == 0), stop=(kc == 21)),
                    reads=[t_w, t_hid[kc]], writes=[t_Y])
            copy_op(P, "act", yT[:, 2 * n + ci, :], bY[:, 0:MT], [t_Y], [t_y[2 * n + ci]])


def transpose_in(cx, src_tok, xT, t_x, xin, t_xin, ntiles=4):
    P = cx.P
    t_xin = list(t_xin)
    P.dma("sp", xin[:, 0:ntiles, :], src_tok.rearrange("(i p) f -> p i f", p=128), writes=t_xin)
    for c in range(8):
        bank, t_b = cx.next_bank()
        for i in range(ntiles):
            P.op("pe", lambda e, c=c, i=i, bank=bank: e.transpose(
                out=bank[:, i * 128:(i + 1) * 128], in_=xin[:, i, c * 128:(c + 1) * 128], identity=cx.ident_f[:]),
                reads=t_xin + [cx.t_const], writes=[t_b])
        copy_op(P, cx.evac_engine(), xT[:, c, 0:ntiles * 128], bank[:, 0:ntiles * 128], [t_b], [t_x[c]])


def load_norm_vecs(P, W, names, l):
    out = {}
    for nm, half in names:
        t = P.sbuf("g_%s_%d" % (nm, l), [128, 8], F32)
        tr = Trk(nm)
        load_vec_fm(P, t[:], W[nm][l], tr)
        if half:
            P.op("dve", lambda e, t=t: e.tensor_scalar(out=t[:], in0=t[:], scalar1=0.5, scalar2=None, op0=ALU.mult),
                 reads=[tr], writes=[tr])
        out[nm] = (t, tr)
    return out


def build_rope(cx, W, ap_=None, aa=None):
    P = cx.P
    if ap_ is None:
        ap_ = palloc(P)
    if aa is None:
        aa = ap_
    posi = aa("posi", [128, 32], I32)
    posf = aa("posf", [128, 32], F32)
    ang = aa("ang", [128, 32, 8], F32)
    ki = aa("rope_ki", [128, 32, 8], I32)
    kf = aa("rope_kf", [128, 32, 8], F32)
    tmp = aa("rope_tmp", [128, 32, 8], F32)
    cos = ap_("cos", [128, 32, 8], F32)
    sin = ap_("sin", [128, 32, 8], F32)
    invf = aa("invf_sb", [128, 8], F32)
    t = Trk("rope")
    R, Wt = [t], [t]
    P.dma("sp", posi[:], W["pos"], writes=Wt)
    P.dma("sp", invf[:], W["invf"], writes=Wt)
    P.op("dve", lambda e: e.tensor_copy(out=posf[:], in_=posi[:]), reads=R, writes=Wt)
    for f in range(8):
        P.op("dve", lambda e, f=f: e.tensor_scalar(out=ang[:, :, f], in0=posf[:], scalar1=invf[:, f:f + 1],
                                                  scalar2=None, op0=ALU.mult), reads=R, writes=Wt)
    C1 = 6.28125
    C2 = 2.0 * np.pi - C1
    PI = float(np.pi)
    P.op("dve", lambda e: e.tensor_scalar(out=kf[:], in0=ang[:], scalar1=float(1.0 / (2.0 * np.pi)), scalar2=None,
                                          op0=ALU.mult), reads=R, writes=Wt)
    P.op("dve", lambda e: e.tensor_copy(out=ki[:], in_=kf[:]), reads=R, writes=Wt)
    P.op("dve", lambda e: e.tensor_copy(out=kf[:], in_=ki[:]), reads=R, writes=Wt)
    P.op("dve", lambda e: e.scalar_tensor_tensor(out=ang[:], in0=kf[:], scalar=float(-C1), in1=ang[:],
                                                 op0=ALU.mult, op1=ALU.add), reads=R, writes=Wt)
    P.op("dve", lambda e: e.scalar_tensor_tensor(out=ang[:], in0=kf[:], scalar=float(-C2), in1=ang[:],
                                                 op0=ALU.mult, op1=ALU.add), reads=R, writes=Wt)

    def wrap(dst, src, shift):
        P.op("dve", lambda e, dst=dst, src=src: e.tensor_scalar(out=dst[:], in0=src[:], scalar1=float(shift),
                                                                scalar2=None, op0=ALU.add), reads=R, writes=Wt)
        P.op("dve", lambda e, dst=dst: e.tensor_scalar(out=tmp[:], in0=dst[:], scalar1=PI, scalar2=float(-2.0 * np.pi),
                                                       op0=ALU.is_gt, op1=ALU.mult), reads=R, writes=Wt)
        P.op("dve", lambda e, dst=dst: e.tensor_tensor(out=dst[:], in0=dst[:], in1=tmp[:], op=ALU.add),
             reads=R, writes=Wt)
        P.op("dve", lambda e, dst=dst: e.tensor_scalar(out=tmp[:], in0=dst[:], scalar1=-PI, scalar2=float(2.0 * np.pi),
                                                       op0=ALU.is_lt, op1=ALU.mult), reads=R, writes=Wt)
        P.op("dve", lambda e, dst=dst: e.tensor_tensor(out=dst[:], in0=dst[:], in1=tmp[:], op=ALU.add),
             reads=R, writes=Wt)
        P.op("dve", lambda e, dst=dst: e.tensor_scalar(out=dst[:], in0=dst[:], scalar1=float(PI - 1e-5),
                                                       scalar2=float(-PI + 1e-5), op0=ALU.min, op1=ALU.max),
             reads=R, writes=Wt)
    wrap(sin, ang, 0.0)
    wrap(cos, sin, PI / 2.0)
    P.op("act", lambda e: e.activation(out=sin[:], in_=sin[:], func=AF.Sin), reads=R, writes=Wt)
    P.op("act", lambda e: e.activation(out=cos[:], in_=cos[:], func=AF.Sin), reads=R, writes=Wt)
    return cos, sin, t


def rope_apply(P, eng, src, dst, nh, cos_s, sin_s, tmp, reads, writes, t_tmp):
    cb = cos_s.unsqueeze(1).to_broadcast([128, nh, 8])
    sb = sin_s.unsqueeze(1).to_broadcast([128, nh, 8])
    t1 = src[:, :, 0:8]
    t2 = src[:, :, 8:16]
    a, b, c, d = tmp[:, 0, 0:nh, :], tmp[:, 1, 0:nh, :], tmp[:, 2, 0:nh, :], tmp[:, 3, 0:nh, :]
    P.op(eng, lambda e: e.tensor_tensor(out=a, in0=t1, in1=cb, op=ALU.mult), reads=reads, writes=[t_tmp])
    P.op(eng, lambda e: e.tensor_tensor(out=b, in0=t2, in1=sb, op=ALU.mult), reads=reads, writes=[t_tmp])
    P.op(eng, lambda e: e.tensor_tensor(out=c, in0=t2, in1=cb, op=ALU.mult), reads=reads, writes=[t_tmp])
    P.op(eng, lambda e: e.tensor_tensor(out=d, in0=t1, in1=sb, op=ALU.mult), reads=reads, writes=[t_tmp])
    P.op(eng, lambda e: e.tensor_tensor(out=dst[:, :, 0:8], in0=a, in1=b, op=ALU.subtract),
         reads=[t_tmp], writes=writes)
    P.op(eng, lambda e: e.tensor_tensor(out=dst[:, :, 8:16], in0=c, in1=d, op=ALU.add),
         reads=[t_tmp], writes=writes)


class Arena:
    def __init__(self, P, name, nbytes):
        self.t = P.sbuf(name, [128, nbytes // 2], BF16)
        self.cap = nbytes // 2
        self.off = 0

    def reset(self):
        self.off = 0

    def __call__(self, name, shape, dt):
        n = 1
        for x in shape[1:]:
            n *= x
        units = n * (2 if dt in (F32, I32) else 1)
        units = (units + 1) // 2 * 2
        v = self.t[0:shape[0], self.off:self.off + units]
        self.off += units
        assert self.off <= self.cap, (name, self.off, self.cap)
        if dt != BF16:
            v = v.bitcast(dt)
        if units != n * (2 if dt in (F32, I32) else 1):
            v = v[:, 0:n]
        if len(shape) == 2:
            return v
        names = "abcd"[:len(shape) - 1]
        pat = "p (%s) -> p %s" % (" ".join(names), " ".join(names))
        kw = {names[i]: shape[1 + i] for i in range(len(shape) - 2)}
        return v.rearrange(pat, **kw)


def palloc_(P):
    def f(name, shape, dt):
        return P.sbuf(name, shape, dt)[:]
    return f


class Dense:
    def __init__(self, alloc_p, alloc_a):
        self.hT = alloc_p("hT", [128, 8, MT], BF16)
        self.t_h = trks(8, "h")
        self.xT = alloc_a("xT", [128, 8, MT], F32)
        self.t_x = trks(8, "x")
        self.hid = alloc_a("hid", [128, 22, MT], BF16)
        self.t_hid = trks(22, "hid")
        self.yT = alloc_a("yT", [128, 8, MT], F32)
        self.t_y = trks(8, "y")


class KVState:
    def __init__(self, alloc_p, alloc_a):
        self.kvf = alloc_a("kvf", [128, 768], F32)
        self.t_kvf = Trk("kvf")
        self.kvb = alloc_a("kvb", [128, 768], BF16)
        self.t_kvb = Trk("kvb")
        self.ktT = alloc_a("ktT", [128, 4, MT], BF16)
        self.t_ktT = Trk("ktT")
        self.vtk = alloc_a("vtk", [128, 4, 2, 128], BF16)
        self.t_vtk = Trk("vtk")
        self.rtmp = alloc_p("rtmp", [128, 4, 8, 8], F32)
        self.t_rtmp = Trk("rtmp")
        self.cuh = alloc_a("cuh", [128, 4, 32, 2], BF16)
        self.t_cuh = Trk("cuh")
        self.utmp = alloc_a("utmp", [128, 32], F32)
        self.t_utmp = Trk("utmp")


def stage_A_tail(cx, scr, dn, kv, l, m, rope, g_mix_pre, out_aps, t_out):
    P, W = cx.P, cx.W
    cos, sin, t_rope = rope
    xs, XK, XV = out_aps["xs"], out_aps["XK"], out_aps["XV"]
    P.dma("sp", xs[m], dn.xT, reads=dn.t_x, writes=[t_out["xs"]])
    rmsnorm_fm(cx, scr, dn.xT, dn.t_x, g_mix_pre[0], g_mix_pre[1], dn.hT, dn.t_h)
    wmix = W["w_mix_in"][l].rearrange("(kc p) n -> p kc n", p=128)
    wkv, t_wkv = cx.load_panel(wmix[:, :, C_KV:C_KV + 768], [128, 8, 768])
    for i in range(4):
        slot = 4 * m + i
        b1, t_b1 = cx.next_bank()
        b2, t_b2 = cx.next_bank()
        for kc in range(8):
            P.op("pe", lambda e, kc=kc, i=i, b1=b1: e.matmul(
                b1[:, 0:512], dn.hT[:, kc, i * 128:(i + 1) * 128], wkv[:, kc, 0:512], start=(kc == 0), stop=(kc == 7)),
                reads=[t_wkv, dn.t_h[kc]], writes=[t_b1])
        for kc in range(8):
            P.op("pe", lambda e, kc=kc, i=i, b2=b2: e.matmul(
                b2[:, 0:256], dn.hT[:, kc, i * 128:(i + 1) * 128], wkv[:, kc, 512:768], start=(kc == 0), stop=(kc == 7)),
                reads=[t_wkv, dn.t_h[kc]], writes=[t_b2])
        copy_op(P, "act", kv.kvf[:, 0:512], b1[:, 0:512], [t_b1], [kv.t_kvf])
        copy_op(P, "dve", kv.kvf[:, 512:768], b2[:, 0:256], [t_b2], [kv.t_kvf])
        copy_op(P, "dve", kv.kvb[:], kv.kvf[:], [kv.t_kvf], [kv.t_kvb])
        for c0 in (256, 512):
            rope_apply(P, "dve", kv.kvf[:, c0:c0 + 128].rearrange("p (g d) -> p g d", g=2),
                       kv.kvb[:, c0:c0 + 128].rearrange("p (g d) -> p g d", g=2), 2,
                       cos[:, slot, :], sin[:, slot, :], kv.rtmp, [kv.t_kvf, t_rope], [kv.t_kvb], kv.t_rtmp)
        bt, t_bt = cx.next_bank()
        tb = bt[:].bitcast(BF16)
        for k, c0 in enumerate((0, 128, 256, 512)):
            P.op("pe", lambda e, k=k, c0=c0, tb=tb: e.transpose(
                out=tb[:, k * 128:(k + 1) * 128], in_=kv.kvb[:, c0:c0 + 128], identity=cx.ident_b[:]),
                reads=[kv.t_kvb, cx.t_const], writes=[t_bt])
        copy_op(P, "act", kv.ktT[:, :, i * 128:(i + 1) * 128],
                tb[:, 0:512].rearrange("p (k t) -> p k t", k=4), [t_bt], [kv.t_ktT])
        copy_op(P, "pool", kv.vtk[:, i, 0, :], kv.kvb[:, 384:512], [kv.t_kvb], [kv.t_vtk])
        copy_op(P, "pool", kv.vtk[:, i, 1, :], kv.kvb[:, 640:768], [kv.t_kvb], [kv.t_vtk])
    P.dma("sp", XK.rearrange("k p t -> p k t")[:, :, m * MT:(m + 1) * MT], kv.ktT[:],
          reads=[kv.t_ktT], writes=[t_out["XK"]])
    for k in range(2):
        P.dma("sp", XV[k].rearrange("(j p) d -> p j d", p=128)[:, 4 * m:4 * m + 4, :], kv.vtk[:, :, k, :],
              reads=[kv.t_vtk], writes=[t_out["XV"]])
    hcols = [dn.hT[:, kc, :].rearrange("p (i t) -> p i t", t=128)[:, :, 126:128] for kc in range(8)]
    wu, t_wu = cx.load_panel(wmix[:, :, C_U:C_U + 512], [128, 8, 512])
    wc, t_wc = cx.load_panel(wmix[:, :, C_C:C_C + 512], [128, 8, 512])
    bh, t_bh = cx.next_bank()
    for (wv, t_wv, off) in ((wu, t_wu, 0), (wc, t_wc, 32)):
        for ch in range(4):
            for kc in range(8):
                P.op("pe", lambda e, wv=wv, ch=ch, kc=kc, off=off: e.matmul(
                    bh[:, off + ch * 8:off + ch * 8 + 8].rearrange("p (i t) -> p i t", t=2),
                    wv[:, kc, ch * 128:(ch + 1) * 128], hcols[kc], start=(kc == 0), stop=(kc == 7)),
                    reads=[t_wv, dn.t_h[kc]], writes=[t_bh])
    copy_op(P, "act", kv.utmp[:], bh[:, 0:32], [t_bh], [kv.t_utmp])
    P.op("dve", lambda e: e.tensor_tensor(
        out=kv.cuh[:, :, 4 * m:4 * m + 4, :], in0=kv.utmp.rearrange("p (c i t) -> p c i t", c=4, i=4),
        in1=bh[:, 32:64].rearrange("p (c i t) -> p c i t", c=4, i=4), op=ALU.mult),
        reads=[kv.t_utmp, t_bh], writes=[kv.t_cuh])


def stage_A_ffn1(cx, scr, dn, l, gv):
    W = cx.W
    rmsnorm_fm(cx, scr, dn.xT, dn.t_x, gv["ffn1_norm_pre"][0], gv["ffn1_norm_pre"][1], dn.hT, dn.t_h)
    ffn(cx, scr, dn.hT, dn.t_h, W["ffn1_w_in"][l], W["ffn1_w_out"][l], dn.hid, dn.t_hid, dn.yT, dn.t_y)
    sandwich_residual(cx, scr, dn.yT, dn.t_y, dn.xT, dn.t_x, gv["ffn1_norm_post"][0], gv["ffn1_norm_post"][1])


WSHAPES = {
    "ffn1_norm_pre": [D], "ffn1_norm_post": [D], "ffn1_w_in": [D, 2 * DFF], "ffn1_w_out": [DFF, D],
    "mix_norm_pre": [D], "mix_norm_post": [D], "mem_norm": [D], "w_mix_in": [D, MIX_IN],
    "conv_w": [3, 512], "cmp_pos_k": [32, 64], "cmp_pos_v": [32, 64],
    "cmp_k_w1": [2048, 256], "cmp_k_b1": [256], "cmp_k_w2": [256, 64],
    "cmp_v_w1": [2048, 256], "cmp_v_b1": [256], "cmp_v_w2": [256, 64],
    "w_mem_kv": [D, D], "w_branch_conv": [512, D], "w_branch_nsa": [512, D], "w_branch_mem": [512, D],
    "w_mix_out": [D, D], "ffn2_norm_pre": [D], "ffn2_norm_post": [D], "ffn2_w_in": [D, 2 * DFF],
    "ffn2_w_out": [DFF, D],
}
A_WEIGHTS = ["ffn1_norm_pre", "ffn1_norm_post", "ffn1_w_in", "ffn1_w_out", "mix_norm_pre", "w_mix_in"]


class WDict(dict):
    pass


def declare_weights(nc, names_layers):
    W = WDict()
    for nm, l in names_layers:
        ap = nc.dram_tensor("%s_%d" % (nm, l), WSHAPES[nm], F32, kind="ExternalInput").ap()
        W.setdefault(nm, {})[l] = ap
    return W


def declare_consts(nc, W):
    W["ident"] = nc.dram_tensor("ident", [128, 128], F32, kind="ExternalInput").ap()
    W["pos"] = nc.dram_tensor("pos", [128, 32], I32, kind="ExternalInput").ap()
    W["invf"] = nc.dram_tensor("invf", [128, 8], F32, kind="ExternalInput").ap()


def build_S0():
    nc = bass.Bass("TRN2", target_bir_lowering=False)
    W = declare_weights(nc, [(n, 0) for n in A_WEIGHTS])
    declare_consts(nc, W)
    x_in = nc.dram_tensor("x_in", [NT, D], F32, kind="ExternalInput").ap()
    outs = {
        "xs": nc.dram_tensor("xs", [NMT, 128, 8, MT], F32, kind="ExternalOutput").ap(),
        "XK": nc.dram_tensor("XK", [4, 128, NT], BF16, kind="ExternalOutput").ap(),
        "XV": nc.dram_tensor("XV", [2, NT, 128], BF16, kind="ExternalOutput").ap(),
        "XH": nc.dram_tensor("XH", [128, 256], BF16, kind="ExternalOutput").ap(),
    }
    t_out = {k: Trk(k) for k in outs}
    P = Prog(nc)
    cx = Ctx(P, W)
    scr = Scr(P)
    dn = Dense(palloc(P), palloc(P))
    kv = KVState(palloc(P), palloc(P))
    rope = build_rope(cx, W)
    gv = load_norm_vecs(P, W, [("ffn1_norm_pre", False), ("ffn1_norm_post", True), ("mix_norm_pre", False)], 0)
    xin = dn.yT.rearrange("p c t -> p (c t)").rearrange("p (i f) -> p i f", i=4)
    for m in range(NMT):
        transpose_in(cx, x_in[m * MT:(m + 1) * MT, :], dn.xT, dn.t_x, xin, dn.t_y)
        stage_A_ffn1(cx, scr, dn, 0, gv)
        stage_A_tail(cx, scr, dn, kv, 0, m, rope, gv["mix_norm_pre"], outs, t_out)
    P.dma("sp", outs["XH"], kv.cuh.rearrange("p c s t -> p (c s t)"), reads=[kv.t_cuh], writes=[t_out["XH"]])
    finals = [t_out[k].w for k in outs]
    n = P.emit(final_waits=finals)
    print("S0 instrs", n, "sbuf KB", P.sb_bytes / 1024, "dma sems", len(P.dma_sem_of))
    P.close()
    return nc


def core_consts(c):
    b, r = c // 4, c % 4
    ident = np.eye(128, dtype=np.float32)
    invf = (500000.0 ** (-np.arange(0, 16, 2, dtype=np.float32) / 16.0)).astype(np.float32)
    invf = np.ascontiguousarray(np.broadcast_to(invf[None, :], (128, 8)))
    return ident, invf


def shard_tokens(a, r):
    t = a.reshape(128, 128, *a.shape[1:])
    return np.ascontiguousarray(t[r::4].reshape(4096, *a.shape[1:]))


def declare_consts_B(nc, W):
    for nm, shp in (("tqneg", [128, 32]), ("curtab", [128, 32]), ("D0", [128, 128]), ("D1", [128, 128]),
                    ("blkiota", [128, 256]), ("blk0", [128, 256]), ("overlap", [1024, 256]),
                    ("Fmat", [128, 2048]), ("onehot", [128, 4])):
        W[nm] = nc.dram_tensor(nm, shp, F32, kind="ExternalInput").ap()


class B1State:
    def __init__(self, cx, ap_, aa):
        P, W = cx.P, cx.W
        self.xT1 = aa("xT1", [128, 8, 128], F32)
        self.t_x1 = trks(8, "x1")
        self.KsT = aa("KsT", [128, 4, NT], BF16)
        self.t_KsT = Trk("KsT")
        self.VsA = aa("VsA", [128, 128, 2, 65], BF16)
        self.t_VsA = Trk("VsA")
        self.VcA = aa("VcA", [128, 8, 2, 321], BF16)
        self.t_VcA = Trk("VcA")
        self.KcT = aa("KcT", [128, 1024], BF16)
        self.t_KcT = Trk("KcT")
        self.Fm = aa("Fm", [128, 2048], BF16)
        self.KC = aa("KC", [128, 4160], BF16)
        self.PT = self.KC[:, 0:4096].rearrange("p (c x) -> p c x", c=8)
        self.t_PT = trks(8, "PT")
        self.wq = aa("wq", [128, 8, 512], BF16)
        self.wg = aa("wg", [128, 8, 24], BF16)
        w1b = aa("w1b", [128, 16, 256], BF16)
        self.w1 = [w1b, w1b]
        self.t_w = Trk("b1w")
        self.z = aa("cz", [128, 256], F32)
        self.zh = aa("czh", [128, 256], F32)
        self.z2 = aa("cz2", [128, 256], F32)
        self.th = aa("cth", [128, 256], F32)
        self.t_z = Trk("cz")
        self.hact = aa("hact", [128, 2, 256], BF16)
        self.t_hact = Trk("hact")
        self.posb = aa("posb", [128, 2, 16], BF16)
        self.b1t = aa("b1t", [128, 2, 2], F32)
        self.b1e = aa("b1e", [128, 2, 2], F32)
        self.w2d = aa("w2d", [128, 2, 128], BF16)
        self.w2v = aa("w2v", [128, 2, 64], BF16)
        self.t_cw = Trk("cw")
        self.KwT = [aa("KwT%d" % i, [128, 8, 128], BF16) for i in range(2)]
        self.VwA = [aa("VwA%d" % i, [128, 8, 2, 65], BF16) for i in range(2)]
        self.t_Kw = trks(2, "Kw")
        self.t_Vw = trks(2, "Vw")
        self.qf = aa("qf", [128, 512], F32)
        self.t_qf = Trk("qf")
        self.qb = aa("qb", [128, 512], BF16)
        self.qr = aa("qr", [128, 512], BF16)
        self.t_qb = Trk("qb")
        self.t_qr = Trk("qr")
        self.gts = aa("gts", [128, 24], F32)
        self.t_gts = Trk("gts")
        self.QTp = aa("QTp", [128, 512], BF16)
        self.QTr = aa("QTr", [128, 512], BF16)
        self.t_QTp = Trk("QTp")
        self.t_QTr = Trk("QTr")
        self.PTs = [aa("PTs%d" % i, [128, 512], BF16) for i in range(3)]
        self.t_PTs = trks(3, "PTs")
        self.pts_i = 0
        self.NMc = [aa("NMc%d" % i, [128, 128], BF16) for i in range(2)]
        self.t_NMc = trks(2, "NMc")
        self.NMw = [aa("NMw%d" % i, [128, 128], BF16) for i in range(8)]
        self.t_NMw = trks(8, "NMw")
        self.vt = aa("vtmp", [128, 128], F32)
        self.t_vt = Trk("vt")
        self.imp = aa("imp", [128, 2, 256], F32)
        self.t_imp = trks(2, "imp")
        self.impm = aa("impm", [128, 256], F32)
        self.imp2 = aa("imp2", [128, 256], F32)
        self.t_impm = Trk("impm")
        self.m8 = aa("m8", [128, 16], F32)
        self.thr = aa("thr", [128, 2], F32)
        self.NMs = aa("NMs", [128, 256], BF16)
        self.t_NMs = Trk("NMs")
        self.NMT = [aa("NMT%d" % g, [128, 3, 128], BF16) for g in range(2)]
        self.t_NMT = trks(2, "NMT")
        self.dm = aa("dm", [128, 256], F32)
        self.forced = aa("forced", [128, 256], F32)
        self.tmpm = aa("tmpm", [128, 256], F32)
        self.keep = aa("keep", [128, 256], F32)
        self.addc = aa("addc", [128, 256], F32)
        self.t_mk = Trk("mk")
        self.och = aa("och", [128, 8, 64], F32)
        self.t_och = Trk("och")
        self.rc1 = aa("rc1", [128, 2], F32)
        self.t_rc1 = Trk("rc1")
        self.y = aa("ytok", [128, 8, 64], F32)
        self.ytmp = aa("ytmp", [128, 4, 64], F32)
        self.t_y = Trk("ytok")
        self.rs4 = aa("rs4", [128, 4], F32)
        self.cf4 = aa("cf4", [128, 4], F32)
        self.t_cf = Trk("cf")
        self.yb = aa("yb", [128, 512], BF16)
        self.t_yb = Trk("yb")
        self.ynT = aa("ynT", [128, 4, 128], BF16)
        self.t_ynT = Trk("ynT")
        self.rtmp = aa("rtmp1", [128, 4, 8, 8], F32)
        self.t_rtmp = Trk("rtmp1")
        self.tqneg = ap_("tqneg_sb", [128, 32], F32)
        self.curtab = ap_("curtab_sb", [128, 32], F32)
        self.D0 = aa("D0_sb", [128, 128], F32)
        self.D1 = aa("D1_sb", [128, 128], F32)
        self.blkiota = aa("blkiota_sb", [128, 256], F32)
        self.blk0 = aa("blk0_sb", [128, 256], F32)
        self.t_c = Trk("b1const")
        for dst, nm in ((self.tqneg, "tqneg"), (self.curtab, "curtab"), (self.D0, "D0"), (self.D1, "D1"),
                        (self.blkiota, "blkiota"), (self.blk0, "blk0")):
            P.dma("sp", dst, W[nm], writes=[self.t_c])
        P.dma("pool", self.Fm, W["Fmat"], writes=[self.t_c])

    def next_pts(self):
        i = self.pts_i
        self.pts_i = (i + 1) % 3
        return self.PTs[i], self.t_PTs[i]


def prep_compress(cx, S, l, GK):
    P, W = cx.P, cx.W
    allPT = S.t_PT
    P.op("pool", lambda e: e.memset(S.KC, 0.0), writes=allPT)
    P.op("pool", lambda e: e.memset(S.VcA[:, :, :, 64:65], 1.0), writes=[S.t_VcA])
    for g in range(2):
        P.dma("pool", S.VcA[:, :, g, 65:321], W["overlap"].rearrange("(c p) j -> p c j", p=128), writes=[S.t_VcA])
    for kind, (w1n, b1n, w2n, posn) in enumerate((("cmp_k_w1", "cmp_k_b1", "cmp_k_w2", "cmp_pos_k"),
                                                   ("cmp_v_w1", "cmp_v_b1", "cmp_v_w2", "cmp_pos_v"))):
        w1v = S.w1[kind]
        P.dma("pool", w1v, W[w1n][l].rearrange("(kk p) h -> p kk h", p=128), writes=[S.t_w])
        P.dma("pool", S.posb[:, kind, :], W[posn][l].rearrange("a d -> (a d)").rearrange("(kk p) -> p kk", p=128),
              writes=[S.t_cw], allow_slow_non_contiguous=True)
        P.dma("sp", S.b1t[:, kind, :], W[b1n][l].rearrange("(c p) -> p c", p=128), writes=[S.t_cw],
              allow_slow_non_contiguous=True)
        w2src = W[w2n][l].rearrange("(c p) d -> p c d", p=128)
        if kind == 0:
            P.dma("pool", S.w2d[:, :, 0:64], w2src, writes=[S.t_cw])
            P.dma("pool", S.w2d[:, :, 64:128], w2src, writes=[S.t_cw])
        else:
            P.dma("pool", S.w2v, w2src, writes=[S.t_cw])
        bb, t_bb = cx.next_bank()
        for hc in range(2):
            for kk in range(16):
                P.op("pe", lambda e, hc=hc, kk=kk, w1v=w1v, kind=kind, bb=bb: e.matmul(
                    bb[:, hc:hc + 1], w1v[:, kk, hc * 128:(hc + 1) * 128], S.posb[:, kind, kk:kk + 1],
                    start=(kk == 0), stop=(kk == 15)), reads=[S.t_w, S.t_cw], writes=[t_bb])
        P.op("dve", lambda e, kind=kind, bb=bb: e.tensor_tensor(out=S.b1e[:, kind, :], in0=bb[:, 0:2],
                                                               in1=S.b1t[:, kind, :], op=ALU.add),
             reads=[t_bb, S.t_cw], writes=[S.t_cw])
        for g in range(2):
            for pc in range(4):
                for rr in range(4):
                    src = GK[rr, kind, g * 64:(g + 1) * 64, pc * 1024:(pc + 1) * 1024].rearrange(
                        "p (s t) -> p s t", t=128)
                    lo = S.KC[0:64, 1:4097].rearrange("p (s r t) -> p s r t", r=4, t=128)[:, :, rr, :]
                    hi = S.KC[64:128, 0:4096].rearrange("p (s r t) -> p s r t", r=4, t=128)[:, :, rr, :]
                    P.dma("sp", lo, src, writes=allPT)
                    P.dma("sp", hi, src, writes=allPT)
                if pc < 3:
                    src = GK[0, kind, g * 64:(g + 1) * 64, (pc + 1) * 1024:(pc + 1) * 1024 + 32]
                    P.dma("sp", S.KC[0:64, 4097:4129], src, writes=allPT)
                    P.dma("sp", S.KC[64:128, 4096:4128], src, writes=allPT)
                for hc in range(2):
                    bk, t_bk = cx.next_bank()
                    for kk in range(16):
                        P.op("pe", lambda e, hc=hc, kk=kk, w1v=w1v, bk=bk: e.matmul(
                            bk[:, 0:256], w1v[:, kk, hc * 128:(hc + 1) * 128],
                            S.KC[:, 1 + 2 * kk:1 + 2 * kk + 4096:16], start=(kk == 0), stop=(kk == 15)),
                            reads=[S.t_w] + allPT, writes=[t_bk])
                    P.op("dve", lambda e, bk=bk, kind=kind, hc=hc: e.tensor_scalar(
                        out=S.z, in0=bk[:, 0:256], scalar1=S.b1e[:, kind, hc:hc + 1], scalar2=None, op0=ALU.add),
                        reads=[t_bk, S.t_cw], writes=[S.t_z])
                    P.op("act", lambda e: e.activation(out=S.z2, in_=S.z, func=AF.Square), reads=[S.t_z], writes=[S.t_z])
                    P.op("pool", lambda e: e.tensor_scalar(out=S.zh, in0=S.z, scalar1=0.5, scalar2=None, op0=ALU.mult),
                         reads=[S.t_z], writes=[S.t_z])
                    P.op("dve", lambda e: e.tensor_scalar(out=S.z2, in0=S.z2, scalar1=0.044715, scalar2=1.0,
                                                          op0=ALU.mult, op1=ALU.add), reads=[S.t_z], writes=[S.t_z])
                    P.op("dve", lambda e: e.tensor_tensor(out=S.z2, in0=S.z2, in1=S.z, op=ALU.mult),
                         reads=[S.t_z], writes=[S.t_z])
                    P.op("act", lambda e: e.activation(out=S.th, in_=S.z2, func=AF.Tanh, scale=0.7978845608),
                         reads=[S.t_z], writes=[S.t_z])
                    P.op("dve", lambda e, hc=hc: e.scalar_tensor_tensor(out=S.hact[:, hc, :], in0=S.th, scalar=1.0,
                                                                       in1=S.zh, op0=ALU.add, op1=ALU.mult),
                         reads=[S.t_z], writes=[S.t_hact])
                if kind == 0:
                    b2, t_b2 = cx.next_bank()
                    for hc in range(2):
                        P.op("pe", lambda e, hc=hc, b2=b2: e.matmul(b2[:, 0:256], S.w2d[:, hc, :], S.hact[:, hc, :],
                                                                   start=(hc == 0), stop=(hc == 1)),
                             reads=[S.t_cw, S.t_hact], writes=[t_b2])
                    copy_op(P, "act", S.KcT[g * 64:(g + 1) * 64, pc * 256:(pc + 1) * 256],
                            b2[g * 64:(g + 1) * 64, 0:256], [t_b2], [S.t_KcT])
                else:
                    for nc_ in range(2):
                        b2, t_b2 = cx.next_bank()
                        for hc in range(2):
                            P.op("pe", lambda e, hc=hc, b2=b2, nc_=nc_: e.matmul(
                                b2[:, 0:64], S.hact[:, hc, nc_ * 128:(nc_ + 1) * 128], S.w2v[:, hc, :],
                                start=(hc == 0), stop=(hc == 1)), reads=[S.t_cw, S.t_hact], writes=[t_b2])
                        copy_op(P, "act", S.VcA[:, 2 * pc + nc_, g, 0:64], b2[:, 0:64], [t_b2], [S.t_VcA])


def load_resident_kv(cx, S, GK, GV):
    P = cx.P
    for rr in range(4):
        P.dma("sp", S.KsT[:, rr, :], GK[rr, 2], writes=[S.t_KsT])
        for g in range(2):
            P.dma("sp", S.VsA[:, rr * 32:(rr + 1) * 32, g, 0:64],
                  GV[rr, 0, :, g * 64:(g + 1) * 64].rearrange("(s p) d -> p s d", p=128), writes=[S.t_VsA])
    P.op("pool", lambda e: e.memset(S.VsA[:, :, :, 64:65], 1.0), writes=[S.t_VsA])
    for i in range(2):
        P.op("pool", lambda e, i=i: e.memset(S.VwA[i][:, :, :, 64:65], 1.0), writes=[S.t_Vw[i]])


def attn_tile(cx, S, bs_mms, mask_mms, vsrc, bo, first, last, g, reads_extra):
    P = cx.P
    bs, t_bs = cx.next_bank()
    bsv = bs[:, 0:512].rearrange("p (h q) -> p h q", h=4)
    nmm = 1 + len(mask_mms)
    lhsT, rhs, rd = bs_mms
    P.op("pe", lambda e: e.matmul(bs[:, 0:512], lhsT, rhs, start=True, stop=(nmm == 1)), reads=rd, writes=[t_bs])
    for k, (ml, mr, mrd, kw) in enumerate(mask_mms):
        P.op("pe", lambda e, ml=ml, mr=mr, k=k, kw=kw: e.matmul(bsv, ml, mr, start=False, stop=(k == nmm - 2), **kw),
             reads=mrd, writes=[t_bs])
    pt, t_pt = S.next_pts()
    P.op("act", lambda e: e.activation(out=pt, in_=bs[:, 0:512], func=AF.Exp, scale=0.125), reads=[t_bs], writes=[t_pt])
    bo_ap, t_bo = bo
    v_ap, v_rd = vsrc
    for hg in range(4):
        P.op("pe", lambda e, hg=hg: e.matmul(bo_ap[:, hg * 65:(hg + 1) * 65], pt[:, hg * 128:(hg + 1) * 128], v_ap,
                                             start=(first and hg == 0), stop=(last and hg == 3), skip_group_check=True),
             reads=[t_pt] + v_rd, writes=[t_bo])


def b1_qtile(cx, scr, dn, S, rope, l, j, xs, GK, GV, YN, t_YN, g_pre):
    P, W = cx.P, cx.W
    cos, sin, t_rope = rope
    m, i = j // 4, j % 4
    wbuf = j % 2
    P.dma("sp", S.xT1, xs[m][:, :, i * 128:(i + 1) * 128], writes=S.t_x1)
    if j > 0:
        P.dma("sp", S.KwT[wbuf][:, 0:4, :], GK[:, 3, :, (j - 1) * 128:j * 128].rearrange("r p t -> p r t"),
              writes=[S.t_Kw[wbuf]])
    P.dma("sp", S.KwT[wbuf][:, 4:8, :], GK[:, 3, :, j * 128:(j + 1) * 128].rearrange("r p t -> p r t"),
          writes=[S.t_Kw[wbuf]])
    for g in range(2):
        if j > 0:
            P.dma("sp", S.VwA[wbuf][:, 0:4, g, 0:64],
                  GV[:, 1, (j - 1) * 128:j * 128, g * 64:(g + 1) * 64].rearrange("r p d -> p r d"),
                  writes=[S.t_Vw[wbuf]])
        P.dma("sp", S.VwA[wbuf][:, 4:8, g, 0:64],
              GV[:, 1, j * 128:(j + 1) * 128, g * 64:(g + 1) * 64].rearrange("r p d -> p r d"),
              writes=[S.t_Vw[wbuf]])
    rmsnorm_fm(cx, scr, S.xT1, S.t_x1, g_pre[0], g_pre[1], dn.hT, dn.t_h, ntok=128)
    bq, t_bq = cx.next_bank()
    bg, t_bg = cx.next_bank()
    for kc in range(8):
        P.op("pe", lambda e, kc=kc: e.matmul(bq[:, 0:512], dn.hT[:, kc, 0:128], S.wq[:, kc, :],
                                             start=(kc == 0), stop=(kc == 7)), reads=[dn.t_h[kc], S.t_w], writes=[t_bq])
    for kc in range(8):
        P.op("pe", lambda e, kc=kc: e.matmul(bg[:, 0:24], dn.hT[:, kc, 0:128], S.wg[:, kc, :],
                                             start=(kc == 0), stop=(kc == 7)), reads=[dn.t_h[kc], S.t_w], writes=[t_bg])
    copy_op(P, "act", S.qf.rearrange("p (h g d) -> p h g d", h=4, g=2),
            bq[:, 0:512].rearrange("p (g h d) -> p h g d", g=2, h=4), [t_bq], [S.t_qf])
    P.op("act", lambda e: e.activation(out=S.gts, in_=bg[:, 0:24], func=AF.Sigmoid), reads=[t_bg], writes=[S.t_gts])
    copy_op(P, "dve", S.qb, S.qf, [S.t_qf], [S.t_qb])
    copy_op(P, "pool", S.qr, S.qf, [S.t_qf], [S.t_qr])
    rope_apply(P, "dve", S.qf.rearrange("p (h d) -> p h d", h=8), S.qr.rearrange("p (h d) -> p h d", h=8), 8,
               cos[:, j, :], sin[:, j, :], S.rtmp, [S.t_qf, t_rope], [S.t_qr], S.t_rtmp)
    for src, t_src, dst, t_dst in ((S.qb, S.t_qb, S.QTp, S.t_QTp), (S.qr, S.t_qr, S.QTr, S.t_QTr)):
        bt, t_bt = cx.next_bank()
        tb = bt[:].bitcast(BF16)
        for hg in range(4):
            P.op("pe", lambda e, hg=hg, tb=tb, src=src: e.transpose(out=tb[:, hg * 128:(hg + 1) * 128],
                                                                   in_=src[:, hg * 128:(hg + 1) * 128],
                                                                   identity=cx.ident_b[:]),
                 reads=[t_src, cx.t_const], writes=[t_bt])
        copy_op(P, cx.evac_engine(), dst, tb[:, 0:512], [t_bt], [t_dst])
    tq = S.tqneg[:, j:j + 1]

    def gen_mask(dst, t_dst, base, off, cmp_op, thr_v):
        P.op("pool", lambda e: e.tensor_scalar(out=S.vt, in0=base, scalar1=float(off), scalar2=tq,
                                               op0=ALU.add, op1=ALU.add), reads=[S.t_c], writes=[S.t_vt])
        P.op("pool", lambda e: e.tensor_scalar(out=dst, in0=S.vt, scalar1=float(thr_v), scalar2=NEGM,
                                               op0=cmp_op, op1=ALU.mult), reads=[S.t_vt], writes=[t_dst])
    t_max = 4 * j + 3
    Cj = min(8, (8 * t_max + 6) // 128 + 1)
    full_ok = (32 * j - 129) // 128 if j >= 1 else -1
    cmask = {}
    for c in range(Cj):
        if c > full_ok:
            k = len(cmask)
            assert k < 2
            gen_mask(S.NMc[k], S.t_NMc[k], S.D0, 2048 * c, ALU.is_gt, 0.0)
            cmask[c] = k
    for ii in range(8):
        kt = 4 * (j - 1) + ii
        if kt < 0:
            continue
        if ii < 4:
            gen_mask(S.NMw[ii], S.t_NMw[ii], S.D1, 128 * kt, ALU.is_le, -512.0)
        else:
            gen_mask(S.NMw[ii], S.t_NMw[ii], S.D1, 128 * kt, ALU.is_gt, 0.0)
    R, Wm = [S.t_mk, S.t_c], [S.t_mk]
    P.op("pool", lambda e: e.tensor_scalar(out=S.dm, in0=S.blkiota, scalar1=S.curtab[:, j:j + 1], scalar2=None,
                                           op0=ALU.subtract), reads=R, writes=Wm)
    P.op("pool", lambda e: e.tensor_scalar(out=S.forced, in0=S.dm, scalar1=0.0, scalar2=None, op0=ALU.is_equal),
         reads=R, writes=Wm)
    P.op("pool", lambda e: e.tensor_scalar(out=S.tmpm, in0=S.dm, scalar1=-1.0, scalar2=None, op0=ALU.is_equal),
         reads=R, writes=Wm)
    P.op("pool", lambda e: e.tensor_tensor(out=S.forced, in0=S.forced, in1=S.tmpm, op=ALU.add), reads=R, writes=Wm)
    P.op("pool", lambda e: e.tensor_tensor(out=S.forced, in0=S.forced, in1=S.blk0, op=ALU.add), reads=R, writes=Wm)
    P.op("pool", lambda e: e.tensor_scalar(out=S.forced, in0=S.forced, scalar1=1.0, scalar2=None, op0=ALU.min),
         reads=R, writes=Wm)
    P.op("pool", lambda e: e.tensor_scalar(out=S.tmpm, in0=S.dm, scalar1=0.0, scalar2=None, op0=ALU.is_gt),
         reads=R, writes=Wm)
    P.op("pool", lambda e: e.tensor_tensor(out=S.keep, in0=S.forced, in1=S.tmpm, op=ALU.add), reads=R, writes=Wm)
    P.op("pool", lambda e: e.tensor_scalar(out=S.keep, in0=S.keep, scalar1=-1.0, scalar2=1.0, op0=ALU.mult,
                                           op1=ALU.add), reads=R, writes=Wm)
    P.op("pool", lambda e: e.tensor_scalar(out=S.addc, in0=S.forced, scalar1=1e30, scalar2=None, op0=ALU.mult),
         reads=R, writes=Wm)
    P.op("pool", lambda e: e.tensor_scalar(out=S.tmpm, in0=S.tmpm, scalar1=-1e30, scalar2=None, op0=ALU.mult),
         reads=R, writes=Wm)
    P.op("pool", lambda e: e.tensor_tensor(out=S.addc, in0=S.addc, in1=S.tmpm, op=ALU.add), reads=R, writes=Wm)
    gv3 = S.gts.rearrange("p (h b) -> p h b", b=3)
    ident4 = cx.ident_b[:].unsqueeze(1).to_broadcast([128, 4, 128])
    for g in range(2):
        gs = slice(g * 64, (g + 1) * 64)
        for c in range(Cj):
            bs, t_bs = cx.next_bank()
            bsv = bs[:, 0:512].rearrange("p (h q) -> p h q", h=4)
            msk = c in cmask
            kc_ap, qp_ap = S.KcT[gs, c * 128:(c + 1) * 128], S.QTp[gs, :]
            P.op("pe", lambda e, bs=bs, msk=msk, kc_ap=kc_ap, qp_ap=qp_ap: e.matmul(
                bs[:, 0:512], kc_ap, qp_ap, start=True, stop=(not msk)),
                 reads=[S.t_KcT, S.t_QTp], writes=[t_bs])
            if msk:
                k = cmask[c]
                P.op("pe", lambda e, k=k, bsv=bsv: e.matmul(
                    bsv, cx.ident_b[:], S.NMc[k].unsqueeze(1).to_broadcast([128, 4, 128]), start=False, stop=True),
                    reads=[S.t_NMc[k], cx.t_const], writes=[t_bs])
            P.op("act", lambda e, c=c, bs=bs: e.activation(out=S.PT[:, c, :], in_=bs[:, 0:512], func=AF.Exp, scale=0.125),
                 reads=[t_bs], writes=[S.t_PT[c]])
        for hg in range(4):
            h = g * 4 + hg
            bx, t_bx = cx.next_bank()
            for c in range(Cj):
                P.op("pe", lambda e, c=c, hg=hg, bx=bx, g=g: e.matmul(bx[:, 0:321], S.PT[:, c, hg * 128:(hg + 1) * 128],
                                                                      S.VcA[:, c, g, :], start=(c == 0), stop=(c == Cj - 1)),
                     reads=[S.t_PT[c], S.t_VcA], writes=[t_bx])
            P.op("dve", lambda e, bx=bx: e.tensor_scalar(out=S.rc1[:, 0:1], in0=bx[:, 64:65], scalar1=1e-30,
                                                         scalar2=None, op0=ALU.max), reads=[t_bx], writes=[S.t_rc1])
            P.op("dve", lambda e: e.reciprocal(out=S.rc1[:, 0:1], in_=S.rc1[:, 0:1]), reads=[S.t_rc1], writes=[S.t_rc1])
            P.op("dve", lambda e, bx=bx, h=h: e.tensor_scalar(out=S.och[:, h, :], in0=bx[:, 0:64],
                                                              scalar1=S.rc1[:, 0:1], scalar2=None, op0=ALU.mult),
                 reads=[t_bx, S.t_rc1], writes=[S.t_och])
            if hg == 0:
                P.op("dve", lambda e, bx=bx, g=g: e.tensor_scalar(out=S.imp[:, g, :], in0=bx[:, 65:321],
                                                             scalar1=S.rc1[:, 0:1], scalar2=None, op0=ALU.mult),
                     reads=[t_bx, S.t_rc1], writes=[S.t_imp[g]])
            else:
                P.op("dve", lambda e, bx=bx, g=g: e.scalar_tensor_tensor(out=S.imp[:, g, :], in0=bx[:, 65:321],
                                                                    scalar=S.rc1[:, 0:1], in1=S.imp[:, g, :],
                                                                    op0=ALU.mult, op1=ALU.add),
                     reads=[t_bx, S.t_rc1, S.t_imp[g]], writes=[S.t_imp[g]])
        Ri, Wi = [S.t_impm], [S.t_impm]
        P.op("dve", lambda e, g=g: e.tensor_tensor(out=S.impm, in0=S.imp[:, g, :], in1=S.keep, op=ALU.mult),
             reads=[S.t_imp[g], S.t_mk], writes=Wi)
        P.op("dve", lambda e: e.tensor_tensor(out=S.impm, in0=S.impm, in1=S.addc, op=ALU.add),
             reads=Ri + [S.t_mk], writes=Wi)
        P.op("dve", lambda e: e.max(out=S.m8[:, 0:8], in_=S.impm), reads=Ri, writes=Wi)
        P.op("dve", lambda e: e.match_replace(out=S.imp2, in_to_replace=S.m8[:, 0:8], in_values=S.impm,
                                              imm_value=-3e38), reads=Ri, writes=Wi)
        P.op("dve", lambda e: e.max(out=S.m8[:, 8:16], in_=S.imp2), reads=Ri, writes=Wi)
        P.op("dve", lambda e: e.tensor_scalar(out=S.thr[:, 0:1], in0=S.m8[:, 15:16], scalar1=-1e29, scalar2=None,
                                              op0=ALU.max), reads=Ri, writes=Wi)
        P.op("dve", lambda e: e.tensor_scalar(out=S.NMs, in0=S.impm, scalar1=S.thr[:, 0:1], scalar2=NEGM,
                                              op0=ALU.is_lt, op1=ALU.mult), reads=Ri, writes=[S.t_NMs])
        bt, t_bt = cx.next_bank()
        tb = bt[:].bitcast(BF16)
        for pcs, (c0, w) in enumerate(((0, 96), (96, 96), (192, 64))):
            P.op("pe", lambda e, pcs=pcs, c0=c0, w=w, tb=tb: e.transpose(
                out=tb[0:w, pcs * 128:(pcs + 1) * 128], in_=S.NMs[:, c0:c0 + w], identity=cx.ident_b[:]),
                reads=[S.t_NMs, cx.t_const], writes=[t_bt])
        copy_op(P, "act", S.NMT[g][0:96, 0:2, :], tb[0:96, 0:256].rearrange("p (a q) -> p a q", a=2),
                [t_bt], [S.t_NMT[g]])
        copy_op(P, "act", S.NMT[g][0:64, 2, :], tb[0:64, 256:384], [t_bt], [S.t_NMT[g]])
        bo_s = (cx.bank[4 + g], cx.btrk[4 + g])
        nkt = 4 * j + 4
        for kt in range(nkt):
            rk, sl = kt % 4, kt // 4
            blk = 2 * kt
            piece, q32, x0 = blk // 96, (blk % 96) // 32, 64 * (blk % 32)
            masks = [(S.Fm[32 * q32:32 * q32 + 32, x0:x0 + 128],
                      S.NMT[g][32 * q32:32 * q32 + 32, piece, :].unsqueeze(1).to_broadcast([32, 4, 128]),
                      [S.t_NMT[g], S.t_c], {})]
            if kt >= 4 * j:
                ii = 4 + kt - 4 * j
                masks.append((cx.ident_b[:], S.NMw[ii].unsqueeze(1).to_broadcast([128, 4, 128]),
                              [S.t_NMw[ii], cx.t_const], {}))
            attn_tile(cx, S, (S.KsT[gs, rk, sl * 128:(sl + 1) * 128], S.QTr[gs, :], [S.t_KsT, S.t_QTr]),
                      masks, (S.VsA[:, rk * 32 + sl, g, :], [S.t_VsA]), bo_s, kt == 0, kt == nkt - 1, g, [])
        bo_w = (cx.bank[6 + g], cx.btrk[6 + g])
        wt = [ii for ii in range(8) if 4 * (j - 1) + ii >= 0]
        for n_, ii in enumerate(wt):
            masks = [(cx.ident_b[:], S.NMw[ii].unsqueeze(1).to_broadcast([128, 4, 128]),
                      [S.t_NMw[ii], cx.t_const], {})]
            attn_tile(cx, S, (S.KwT[wbuf][gs, ii, :], S.QTr[gs, :], [S.t_Kw[wbuf], S.t_QTr]),
                      masks, (S.VwA[wbuf][:, ii, g, :], [S.t_Vw[wbuf]]), bo_w, n_ == 0, n_ == len(wt) - 1, g, [])
    P.op("dve", lambda e: e.tensor_tensor(out=S.y, in0=S.och, in1=gv3[:, :, 0:1].to_broadcast([128, 8, 64]),
                                          op=ALU.mult), reads=[S.t_och, S.t_gts], writes=[S.t_y])
    for g in range(2):
        for br, bnk in ((1, 4 + g), (2, 6 + g)):
            bo, t_bo = cx.bank[bnk], cx.btrk[bnk]
            bov = bo[:, 0:260].rearrange("p (h x) -> p h x", x=65)
            P.op("dve", lambda e, bov=bov: e.tensor_scalar(out=S.rs4, in0=bov[:, :, 64], scalar1=1e-30, scalar2=None,
                                                           op0=ALU.max), reads=[t_bo], writes=[S.t_cf])
            P.op("dve", lambda e: e.reciprocal(out=S.rs4, in_=S.rs4), reads=[S.t_cf], writes=[S.t_cf])
            P.op("dve", lambda e, g=g, br=br: e.tensor_tensor(out=S.cf4, in0=S.rs4, in1=gv3[:, g * 4:(g + 1) * 4, br],
                                                              op=ALU.mult), reads=[S.t_cf, S.t_gts], writes=[S.t_cf])
            P.op("dve", lambda e, bov=bov: e.tensor_tensor(out=S.ytmp, in0=bov[:, :, 0:64],
                                                           in1=S.cf4.unsqueeze(2).to_broadcast([128, 4, 64]),
                                                           op=ALU.mult), reads=[t_bo, S.t_cf], writes=[S.t_cf])
            P.op("dve", lambda e, g=g: e.tensor_tensor(out=S.y[:, g * 4:(g + 1) * 4, :], in0=S.y[:, g * 4:(g + 1) * 4, :],
                                                       in1=S.ytmp, op=ALU.add), reads=[S.t_cf, S.t_y], writes=[S.t_y])
    copy_op(P, "act", S.yb, S.y.rearrange("p h d -> p (h d)"), [S.t_y], [S.t_yb])
    bt, t_bt = cx.next_bank()
    tb = bt[:].bitcast(BF16)
    for ch in range(4):
        P.op("pe", lambda e, ch=ch, tb=tb: e.transpose(out=tb[:, ch * 128:(ch + 1) * 128],
                                                       in_=S.yb[:, ch * 128:(ch + 1) * 128], identity=cx.ident_b[:]),
             reads=[S.t_yb, cx.t_const], writes=[t_bt])
    copy_op(P, "dve", S.ynT, tb[:, 0:512].rearrange("p (c t) -> p c t", c=4), [t_bt], [S.t_ynT])
    P.dma("sp", YN[j], S.ynT, reads=[S.t_ynT], writes=[t_YN])


class B2State:
    def __init__(self, cx, ap_, aa):
        P, W = cx.P, cx.W
        self.KmT = ap_("KmT", [128, 4, 256], BF16)
        self.VmA = ap_("VmA", [128, 2, 512], BF16)
        self.t_mem = Trk("memkv")
        self.convw = ap_("convw", [128, 4, 3], F32)
        self.onehot = ap_("onehot_sb", [128, 4], F32)
        self.GHsb = aa("GHsb", [128, 4, 256], BF16)
        self.t_cst = Trk("b2const")
        self.memT = aa("memT", [128, 8, 256], F32)
        self.t_memT = trks(8, "memT")
        self.memn = aa("memn", [128, 8, 256], BF16)
        self.t_memn = trks(8, "memn")
        self.memin = aa("memin", [128, 2, 1024], F32)
        self.t_memin = Trk("memin")
        self.cu = aa("cu", [128, 4, 130], F32)
        self.t_cu = Trk("cu")
        self.usb = aa("usb", [128, 512], F32)
        self.t_usb = Trk("usb")
        self.acc = aa("acc", [128, 512], F32)
        self.t_acc = Trk("acc")
        self.ycT = aa("ycT", [128, 4, MT], BF16)
        self.ymT = aa("ymT", [128, 4, MT], BF16)
        self.ynT = aa("ynT2", [128, 4, MT], BF16)
        self.t_yc = trks(4, "yc")
        self.t_ym = trks(4, "ym")
        self.t_yn = trks(4, "yn")
        self.qmT = aa("qmT", [128, 4, MT], BF16)
        self.t_qm = trks(4, "qm")
        self.PTm = aa("PTm", [128, 2, MT], BF16)
        self.t_PTm = trks(2, "PTm")
        self.rr = aa("rr", [128, MT], F32)
        self.t_rr = Trk("rr")
        self.sg = aa("sg", [128, 2, MT], BF16)
        self.t_sg = trks(2, "sg")
        self.mtmp = aa("mtmp", [128, MT], F32)
        self.t_mtmp = Trk("mtmp")
        self.mT = aa("mT", [128, 8, MT], BF16)
        self.t_mT = trks(8, "mT")


def prep_mem(cx, scr, S2, l, mem_ap, gv_mem):
    P, W = cx.P, cx.W
    P.dma("sp", S2.memin, mem_ap.rearrange("(i p) f -> p i f", p=128), writes=[S2.t_memin])
    for c in range(8):
        bank, t_b = cx.next_bank()
        for i in range(2):
            P.op("pe", lambda e, c=c, i=i, bank=bank: e.transpose(
                out=bank[:, i * 128:(i + 1) * 128], in_=S2.memin[:, i, c * 128:(c + 1) * 128], identity=cx.ident_f[:]),
                reads=[S2.t_memin, cx.t_const], writes=[t_b])
        copy_op(P, cx.evac_engine(), S2.memT[:, c, :], bank[:, 0:256], [t_b], [S2.t_memT[c]])
    rmsnorm_fm(cx, scr, S2.memT, S2.t_memT, gv_mem[0], gv_mem[1], S2.memn, S2.t_memn, ntok=256)
    wv = W["w_mem_kv"][l].rearrange("(kc p) n -> p kc n", p=128)
    wk, t_wk = cx.load_panel(wv[:, :, 0:512], [128, 8, 512])
    for h in range(4):
        bank, t_b = cx.next_bank()
        for kc in range(8):
            P.op("pe", lambda e, h=h, kc=kc, bank=bank: e.matmul(bank[:, 0:256], wk[:, kc, h * 128:(h + 1) * 128],
                                                                 S2.memn[:, kc, :], start=(kc == 0), stop=(kc == 7)),
                 reads=[t_wk, S2.t_memn[kc]], writes=[t_b])
        copy_op(P, cx.evac_engine(), S2.KmT[:, h, :], bank[:, 0:256], [t_b], [S2.t_mem])
    wvv, t_wv = cx.load_panel(wv[:, :, 512:1024], [128, 8, 512])
    for mc in range(2):
        bank, t_b = cx.next_bank()
        for kc in range(8):
            P.op("pe", lambda e, mc=mc, kc=kc, bank=bank: e.matmul(bank[:, 0:512], S2.memn[:, kc, mc * 128:(mc + 1) * 128],
                                                                   wvv[:, kc, :], start=(kc == 0), stop=(kc == 7)),
                 reads=[t_wv, S2.t_memn[kc]], writes=[t_b])
        copy_op(P, cx.evac_engine(), S2.VmA[:, mc, :], bank[:, 0:512], [t_b], [S2.t_mem])


def proj_fm(cx, dn, wsrc, nchunks, consume, kcn=8, rhs=None, t_rhs=None):
    P = cx.P
    if rhs is None:
        rhs, t_rhs = dn.hT, dn.t_h
    wv, t_w = cx.load_panel(wsrc, [128, kcn, nchunks * 128])
    for ci in range(nchunks):
        bank, t_b = cx.next_bank()
        for kc in range(kcn):
            P.op("pe", lambda e, ci=ci, kc=kc, bank=bank: e.matmul(bank[:, 0:MT], wv[:, kc, ci * 128:(ci + 1) * 128],
                                                                   rhs[:, kc, :], start=(kc == 0), stop=(kc == kcn - 1)),
                 reads=[t_w, t_rhs[kc]], writes=[t_b])
        consume(ci, bank, t_b)


def stage_B2(cx, scr, dn, S2, l, m, xs, YN, t_YN, gv):
    P, W = cx.P, cx.W
    wmix = W["w_mix_in"][l].rearrange("(kc p) n -> p kc n", p=128)
    P.dma("sp", dn.xT, xs[m], writes=dn.t_x)
    for i in range(4):
        P.dma("sp", S2.ynT[:, :, i * 128:(i + 1) * 128], YN[4 * m + i], reads=[t_YN], writes=S2.t_yn)
    rmsnorm_fm(cx, scr, dn.xT, dn.t_x, gv["mix_norm_pre"][0], gv["mix_norm_pre"][1], dn.hT, dn.t_h)
    wu, t_wu = cx.load_panel(wmix[:, :, C_U:C_U + 512], [128, 8, 512])
    wc, t_wc = cx.load_panel(wmix[:, :, C_C:C_C + 512], [128, 8, 512])
    wb, t_wb = cx.load_panel(wmix[:, :, C_B:C_B + 512], [128, 8, 512])
    GHv = [S2.GHsb[:, k, :].rearrange("p (c s t) -> p c s t", c=4, s=32) for k in range(4)]
    for ch in range(4):
        bu, t_bu = cx.next_bank()
        bc, t_bc = cx.next_bank()
        for (bank, t_b, wv, t_w) in ((bu, t_bu, wu, t_wu), (bc, t_bc, wc, t_wc)):
            for kc in range(8):
                P.op("pe", lambda e, kc=kc, ch=ch, bank=bank, wv=wv: e.matmul(
                    bank[:, 0:MT], wv[:, kc, ch * 128:(ch + 1) * 128], dn.hT[:, kc, :], start=(kc == 0), stop=(kc == 7)),
                    reads=[t_w, dn.t_h[kc]], writes=[t_b])
        copy_op(P, "act", S2.usb, bu[:, 0:MT], [t_bu], [S2.t_usb])
        P.op("dve", lambda e, bc=bc: e.tensor_tensor(out=S2.cu[:, :, 2:130], in0=S2.usb.rearrange("p (i t) -> p i t", t=128),
                                                     in1=bc[:, 0:MT].rearrange("p (i t) -> p i t", t=128), op=ALU.mult),
             reads=[S2.t_usb, t_bc], writes=[S2.t_cu])
        for i in range(4):
            sl = 4 * m + i
            first = True
            for k in range(4):
                ss = sl if k < 3 else sl - 1
                if ss < 0:
                    continue
                src = GHv[k][:, ch, ss, :]
                if first:
                    P.op("dve", lambda e, i=i, k=k, src=src: e.tensor_scalar(
                        out=S2.cu[:, i, 0:2], in0=src, scalar1=S2.onehot[:, k:k + 1], scalar2=None, op0=ALU.mult),
                        reads=[S2.t_cst], writes=[S2.t_cu])
                    first = False
                else:
                    P.op("dve", lambda e, i=i, k=k, src=src: e.scalar_tensor_tensor(
                        out=S2.cu[:, i, 0:2], in0=src, scalar=S2.onehot[:, k:k + 1], in1=S2.cu[:, i, 0:2],
                        op0=ALU.mult, op1=ALU.add), reads=[S2.t_cst, S2.t_cu], writes=[S2.t_cu])
        accv = S2.acc.rearrange("p (i t) -> p i t", t=128)
        P.op("dve", lambda e, ch=ch: e.tensor_scalar(out=accv, in0=S2.cu[:, :, 2:130], scalar1=S2.convw[:, ch, 2:3],
                                                     scalar2=None, op0=ALU.mult), reads=[S2.t_cu, S2.t_cst], writes=[S2.t_acc])
        for tap, off in ((1, 1), (0, 0)):
            P.op("dve", lambda e, ch=ch, tap=tap, off=off: e.scalar_tensor_tensor(
                out=accv, in0=S2.cu[:, :, off:off + 128], scalar=S2.convw[:, ch, tap:tap + 1], in1=accv,
                op0=ALU.mult, op1=ALU.add), reads=[S2.t_cu, S2.t_cst, S2.t_acc], writes=[S2.t_acc])
        bb, t_bb = cx.next_bank()
        for kc in range(8):
            P.op("pe", lambda e, kc=kc, ch=ch, bb=bb: e.matmul(bb[:, 0:MT], wb[:, kc, ch * 128:(ch + 1) * 128],
                                                               dn.hT[:, kc, :], start=(kc == 0), stop=(kc == 7)),
                 reads=[t_wb, dn.t_h[kc]], writes=[t_bb])
        P.op("dve", lambda e, ch=ch, bb=bb: e.tensor_tensor(out=S2.ycT[:, ch, :], in0=S2.acc, in1=bb[:, 0:MT], op=ALU.mult),
             reads=[S2.t_acc, t_bb], writes=[S2.t_yc[ch]])
    def cons_qm(ci, bank, t_b):
        copy_op(P, cx.evac_engine(), S2.qmT[:, ci, :], bank[:, 0:MT], [t_b], [S2.t_qm[ci]])
    proj_fm(cx, dn, wmix[:, :, C_QM:C_QM + 512], 4, cons_qm)
    for h in range(4):
        for mc in range(2):
            bs, t_bs = cx.next_bank()
            P.op("pe", lambda e, h=h, mc=mc, bs=bs: e.matmul(bs[:, 0:MT], S2.KmT[:, h, mc * 128:(mc + 1) * 128],
                                                             S2.qmT[:, h, :], start=True, stop=True),
                 reads=[S2.t_mem, S2.t_qm[h]], writes=[t_bs])
            P.op("act", lambda e, mc=mc, bs=bs: e.activation(out=S2.PTm[:, mc, :], in_=bs[:, 0:MT], func=AF.Exp,
                                                             scale=float(128.0 ** -0.5)),
                 reads=[t_bs], writes=[S2.t_PTm[mc]])
        by, t_by = cx.next_bank()
        br, t_br = cx.next_bank()
        for mc in range(2):
            P.op("pe", lambda e, h=h, mc=mc, by=by: e.matmul(by[:, 0:MT], S2.VmA[:, mc, h * 128:(h + 1) * 128],
                                                             S2.PTm[:, mc, :], start=(mc == 0), stop=(mc == 1)),
                 reads=[S2.t_mem, S2.t_PTm[mc]], writes=[t_by])
        for mc in range(2):
            P.op("pe", lambda e, mc=mc, br=br: e.matmul(br[:, 0:MT], cx.ones1[:], S2.PTm[:, mc, :],
                                                        start=(mc == 0), stop=(mc == 1)),
                 reads=[cx.t_const, S2.t_PTm[mc]], writes=[t_br])
        P.op("dve", lambda e, br=br: e.reciprocal(out=S2.rr, in_=br[:, 0:MT]), reads=[t_br], writes=[S2.t_rr])
        P.op("dve", lambda e, h=h, by=by: e.tensor_tensor(out=S2.ymT[:, h, :], in0=S2.rr, in1=by[:, 0:MT], op=ALU.mult),
             reads=[S2.t_rr, t_by], writes=[S2.t_ym[h]])
    mF = dn.hid.rearrange("p c t -> p (c t)").bitcast(F32).rearrange("p (c t) -> p c t", c=11)
    branches = (("w_branch_conv", S2.ycT, S2.t_yc), ("w_branch_nsa", S2.ynT, S2.t_yn), ("w_branch_mem", S2.ymT, S2.t_ym))
    for b, (wn, yb, t_yb) in enumerate(branches):
        wbr = W[wn][l].rearrange("(kc p) n -> p kc n", p=128)
        for half in range(2):
            wg_, t_wg = cx.load_panel(wmix[:, :, C_MG + b * 1024 + half * 512:C_MG + b * 1024 + (half + 1) * 512],
                                      [128, 8, 512])
            wbv, t_wbv = cx.load_panel(wbr[:, :, half * 512:(half + 1) * 512], [128, 4, 512])
            for ci in range(4):
                c = half * 4 + ci
                bg, t_bg = cx.next_bank()
                bbr, t_bbr = cx.next_bank()
                for kc in range(8):
                    P.op("pe", lambda e, kc=kc, ci=ci, bg=bg, wg_=wg_: e.matmul(
                        bg[:, 0:MT], wg_[:, kc, ci * 128:(ci + 1) * 128], dn.hT[:, kc, :], start=(kc == 0), stop=(kc == 7)),
                        reads=[t_wg, dn.t_h[kc]], writes=[t_bg])
                for kc in range(4):
                    P.op("pe", lambda e, kc=kc, ci=ci, bbr=bbr, wbv=wbv, yb=yb: e.matmul(
                        bbr[:, 0:MT], wbv[:, kc, ci * 128:(ci + 1) * 128], yb[:, kc, :], start=(kc == 0), stop=(kc == 3)),
                        reads=[t_wbv, t_yb[kc]], writes=[t_bbr])
                s_ = c % 2
                P.op("act", lambda e, bg=bg, s_=s_: e.activation(out=S2.sg[:, s_, :], in_=bg[:, 0:MT], func=AF.Sigmoid),
                     reads=[t_bg], writes=[S2.t_sg[s_]])
                tm = [dn.t_hid[2 * c], dn.t_hid[2 * c + 1]]
                if b == 0:
                    P.op("dve", lambda e, c=c, s_=s_, bbr=bbr: e.tensor_tensor(out=mF[:, c, :], in0=S2.sg[:, s_, :],
                                                                              in1=bbr[:, 0:MT], op=ALU.mult),
                         reads=[S2.t_sg[s_], t_bbr], writes=tm)
                else:
                    P.op("dve", lambda e, s_=s_, bbr=bbr: e.tensor_tensor(out=S2.mtmp, in0=S2.sg[:, s_, :],
                                                                         in1=bbr[:, 0:MT], op=ALU.mult),
                         reads=[S2.t_sg[s_], t_bbr], writes=[S2.t_mtmp])
                    if b == 1:
                        P.op("pool", lambda e, c=c: e.tensor_tensor(out=mF[:, c, :], in0=mF[:, c, :], in1=S2.mtmp, op=ALU.add),
                             reads=[S2.t_mtmp] + tm, writes=tm)
                    else:
                        P.op("pool", lambda e, c=c: e.tensor_tensor(out=S2.mT[:, c, :], in0=mF[:, c, :], in1=S2.mtmp, op=ALU.add),
                             reads=[S2.t_mtmp] + tm, writes=[S2.t_mT[c]])
    wo = W["w_mix_out"][l].rearrange("(kc p) n -> p kc n", p=128)
    for half in range(2):
        def cons_o(ci, bank, t_b, half=half):
            copy_op(P, "act", dn.yT[:, half * 4 + ci, :], bank[:, 0:MT], [t_b], [dn.t_y[half * 4 + ci]])
        proj_fm(cx, dn, wo[:, :, half * 512:(half + 1) * 512], 4, cons_o, rhs=S2.mT, t_rhs=S2.t_mT)
    sandwich_residual(cx, scr, dn.yT, dn.t_y, dn.xT, dn.t_x, gv["mix_norm_post"][0], gv["mix_norm_post"][1])
    rmsnorm_fm(cx, scr, dn.xT, dn.t_x, gv["ffn2_norm_pre"][0], gv["ffn2_norm_pre"][1], dn.hT, dn.t_h)
    ffn(cx, scr, dn.hT, dn.t_h, W["ffn2_w_in"][l], W["ffn2_w_out"][l], dn.hid, dn.t_hid, dn.yT, dn.t_y)
    sandwich_residual(cx, scr, dn.yT, dn.t_y, dn.xT, dn.t_x, gv["ffn2_norm_post"][0], gv["ffn2_norm_post"][1])


def transpose_out(cx, dn, dst_tok, t_dst):
    P = cx.P
    xo = dn.yT.rearrange("p c t -> p (c t)").rearrange("p (i f) -> p i f", i=4)
    for i in range(4):
        for half in range(2):
            bank, t_b = cx.next_bank()
            for cc in range(4):
                c = half * 4 + cc
                P.op("pe", lambda e, i=i, c=c, cc=cc, bank=bank: e.transpose(
                    out=bank[:, cc * 128:(cc + 1) * 128], in_=dn.xT[:, c, i * 128:(i + 1) * 128], identity=cx.ident_f[:]),
                    reads=[dn.t_x[c], cx.t_const], writes=[t_b])
            copy_op(P, cx.evac_engine(), xo[:, i, half * 512:(half + 1) * 512], bank[:, 0:512], [t_b], dn.t_y)
    P.dma("sp", dst_tok.rearrange("(i p) f -> p i f", p=128), xo, reads=dn.t_y, writes=[t_dst])


B_WEIGHTS = ["mix_norm_pre", "mix_norm_post", "mem_norm", "w_mix_in", "conv_w", "cmp_pos_k", "cmp_pos_v",
             "cmp_k_w1", "cmp_k_b1", "cmp_k_w2", "cmp_v_w1", "cmp_v_b1", "cmp_v_w2", "w_mem_kv",
             "w_branch_conv", "w_branch_nsa", "w_branch_mem", "w_mix_out", "ffn2_norm_pre", "ffn2_norm_post",
             "ffn2_w_in", "ffn2_w_out"]
ARENA_BYTES = 164 * 1024


def build_SB(l, last, nslots=32):
    nc = bass.Bass("TRN2", target_bir_lowering=False)
    wl = [(n, l) for n in B_WEIGHTS]
    if not last:
        wl += [(n, l + 1) for n in A_WEIGHTS]
    W = declare_weights(nc, wl)
    declare_consts(nc, W)
    declare_consts_B(nc, W)
    xs = nc.dram_tensor("xs_in", [NMT, 128, 8, MT], F32, kind="ExternalInput").ap()
    GK = nc.dram_tensor("GK", [4, 4, 128, NT], BF16, kind="ExternalInput").ap()
    GV = nc.dram_tensor("GV", [4, 2, NT, 128], BF16, kind="ExternalInput").ap()
    GH = nc.dram_tensor("GH", [4, 128, 256], BF16, kind="ExternalInput").ap()
    mem = nc.dram_tensor("mem", [256, D], F32, kind="ExternalInput").ap()
    YN = nc.dram_tensor("YN", [32, 128, 4, 128], BF16, kind="Internal").ap()
    t_YN = Trk("YN")
    if last:
        outs = {"y_out": nc.dram_tensor("y_out", [NT, D], F32, kind="ExternalOutput").ap()}
    else:
        outs = {
            "xs": nc.dram_tensor("xs", [NMT, 128, 8, MT], F32, kind="ExternalOutput").ap(),
            "XK": nc.dram_tensor("XK", [4, 128, NT], BF16, kind="ExternalOutput").ap(),
            "XV": nc.dram_tensor("XV", [2, NT, 128], BF16, kind="ExternalOutput").ap(),
            "XH": nc.dram_tensor("XH", [128, 256], BF16, kind="ExternalOutput").ap(),
        }
    t_out = {k: Trk(k) for k in outs}
    P = Prog(nc)
    ap_ = palloc(P)
    arena = Arena(P, "arena", ARENA_BYTES)
    cx = Ctx(P, W, wslots=[], wsize=0)
    scr = Scr(P)
    arena.reset()
    cx.set_pool(4)
    cx.set_wslots([], 0)
    S1 = B1State(cx, ap_, arena)
    rope = build_rope(cx, W, ap_, arena)
    hT = ap_("hT", [128, 8, MT], BF16)
    t_h = trks(8, "h")

    class DN:
        pass
    dn1 = DN()
    dn1.hT, dn1.t_h = hT, t_h
    names = [("mix_norm_pre", False), ("mix_norm_post", False), ("mem_norm", False),
             ("ffn2_norm_pre", False), ("ffn2_norm_post", True)]
    gv = load_norm_vecs(P, W, names, l)
    wmix = W["w_mix_in"][l].rearrange("(kc p) n -> p kc n", p=128)
    load_resident_kv(cx, S1, GK, GV)
    prep_compress(cx, S1, l, GK)
    P.dma("pool", S1.wq, wmix[:, :, C_Q:C_Q + 512], writes=[S1.t_w])
    P.dma("pool", S1.wg, wmix[:, :, C_NG:C_NG + 24], writes=[S1.t_w])
    for j in range(nslots):
        b1_qtile(cx, scr, dn1, S1, rope, l, j, xs, GK, GV, YN, t_YN, gv["mix_norm_pre"])
    print("phase1 arena KB", arena.off * 2 / 1024)
    P.barrier()
    arena.reset()
    cx.set_pool(8)
    cx.set_wslots([arena("wslot%d" % i, [128, 6144], BF16) for i in range(3)], 6144)
    dn = Dense(lambda n, s, d: hT, arena)
    dn.t_h = t_h
    kv = KVState(ap_, arena)
    S2 = B2State(cx, ap_, arena)
    print("phase2 arena KB", arena.off * 2 / 1024, "persistent KB", (P.sb_bytes - ARENA_BYTES) / 1024)
    for k in range(3):
        P.dma("sp", S2.convw[:, :, k], W["conv_w"][l][k].rearrange("(c p) -> p c", p=128), writes=[S2.t_cst],
              allow_slow_non_contiguous=True)
    P.dma("sp", S2.onehot, W["onehot"], writes=[S2.t_cst])
    P.dma("sp", S2.GHsb, GH.rearrange("r p x -> p r x"), writes=[S2.t_cst])
    prep_mem(cx, scr, S2, l, mem, gv["mem_norm"])
    if not last:
        gvn = load_norm_vecs(P, W, [("ffn1_norm_pre", False), ("ffn1_norm_post", True), ("mix_norm_pre", False)], l + 1)
    for m in range(nslots // 4):
        stage_B2(cx, scr, dn, S2, l, m, xs, YN, t_YN, gv)
        if last:
            transpose_out(cx, dn, outs["y_out"][m * MT:(m + 1) * MT, :], t_out["y_out"])
        else:
            stage_A_ffn1(cx, scr, dn, l + 1, gvn)
            stage_A_tail(cx, scr, dn, kv, l + 1, m, rope, gvn["mix_norm_pre"], outs, t_out)
    if not last:
        P.dma("sp", outs["XH"], kv.cuh.rearrange("p c s t -> p (c s t)"), reads=[kv.t_cuh], writes=[t_out["XH"]])
    finals = [t_out[k].w for k in outs if t_out[k].w is not None]
    n = P.emit(final_waits=finals)
    print("SB instrs", n, "dma sems", len(P.dma_sem_of))
    P.close()
    return nc


def consts_for_core(c):
    r = c % 4
    j = np.arange(32, dtype=np.float32)
    p = np.arange(128, dtype=np.float32)
    t = 4.0 * j + r
    d = {}
    d["ident"] = np.eye(128, dtype=np.float32)
    invf = (500000.0 ** (-np.arange(0, 16, 2, dtype=np.float32) / 16.0)).astype(np.float32)
    d["invf"] = np.ascontiguousarray(np.broadcast_to(invf[None, :], (128, 8))).astype(np.float32)
    d["tqneg"] = np.ascontiguousarray(np.broadcast_to((-128.0 * t)[None, :], (128, 32))).astype(np.float32)
    d["curtab"] = (2.0 * t[None, :] + (p[:, None] >= 64)).astype(np.float32)
    d["D0"] = (16.0 * p[:, None] + 31.0 - p[None, :]).astype(np.float32)
    d["D1"] = (p[:, None] - p[None, :]).astype(np.float32)
    blk = np.arange(256, dtype=np.float32)
    d["blkiota"] = np.ascontiguousarray(np.broadcast_to(blk[None, :], (128, 256))).astype(np.float32)
    d["blk0"] = np.ascontiguousarray(np.broadcast_to((blk == 0)[None, :], (128, 256))).astype(np.float32)
    n = np.arange(1024)[:, None] * 16
    jj = np.arange(256)[None, :] * 64
    ov = np.maximum(np.minimum(n + 32, jj + 64) - np.maximum(n, jj), 0).astype(np.float32) / 32.0
    ov[1023] = 0.0
    d["overlap"] = ov
    d["Fmat"] = (np.arange(2048)[None, :] // 64 == (np.arange(128)[:, None] % 32)).astype(np.float32)
    oh = np.zeros((128, 4), np.float32)
    if r == 0:
        oh[:, 3] = 1.0
    else:
        oh[:, r - 1] = 1.0
    d["onehot"] = oh
    return d


_PROGS = {}


def _prog(key, fn):
    if key not in _PROGS:
        _PROGS[key] = fn()
    return _PROGS[key]


def _gather(results, b, key):
    return np.ascontiguousarray(np.stack([np.asarray(results[4 * b + rr][key]) for rr in range(4)], axis=0))


def kernel(**inputs):
    inp = {k: np.asarray(v) for k, v in inputs.items()}
    x, mem, positions = inp["x"], inp["mem"], inp["positions"]
    cores = list(range(8))
    cst = [consts_for_core(c) for c in cores]
    pos_c = []
    for c in cores:
        b, r = c // 4, c % 4
        pos_c.append(np.ascontiguousarray(shard_tokens(positions[b].astype(np.int32), r).reshape(32, 128).T))
    wl = {n: [np.ascontiguousarray(inp[n][l], dtype=np.float32) for l in range(2)] for n in WSHAPES}

    nc0 = _prog("S0", build_S0)
    maps = []
    for c in cores:
        b, r = c // 4, c % 4
        m = {"ident": cst[c]["ident"], "invf": cst[c]["invf"], "pos": pos_c[c],
             "x_in": shard_tokens(x[b].astype(np.float32), r)}
        for n in A_WEIGHTS:
            m["%s_0" % n] = wl[n][0]
        maps.append(m)
    res = run_bass_kernel_spmd(nc0, maps, core_ids=cores).results

    for l in range(2):
        last = (l == 1)
        ncl = _prog("SB%d" % l, lambda: build_SB(l, last))
        G = {b: {k: _gather(res, b, k) for k in ("XK", "XV", "XH")} for b in range(2)}
        maps = []
        for c in cores:
            b, r = c // 4, c % 4
            m = dict(cst[c])
            m["pos"] = pos_c[c]
            m["xs_in"] = np.asarray(res[c]["xs"])
            m["GK"], m["GV"], m["GH"] = G[b]["XK"], G[b]["XV"], G[b]["XH"]
            m["mem"] = np.ascontiguousarray(mem[b], dtype=np.float32)
            for n in B_WEIGHTS:
                m["%s_%d" % (n, l)] = wl[n][l]
            if not last:
                for n in A_WEIGHTS:
                    m["%s_%d" % (n, l + 1)] = wl[n][l + 1]
            maps.append(m)
        res = run_bass_kernel_spmd(ncl, maps, core_ids=cores).results
    out = np.zeros((2, 16384, 1024), np.float32)
    for c in cores:
        b, r = c // 4, c % 4
        out[b].reshape(128, 128, 1024)[r::4] = np.asarray(res[c]["y_out"]).reshape(32, 128, 1024)
    if os.environ.get("KDUMP"):
        np.save(os.environ["KDUMP"], out)
    return out
```
